# Optimizing a Trainium2 kernel written in Bass

```python
import jax, jax.numpy as jnp
from jax import lax
import numpy as np

D_MODEL = 1024
BATCH = 16
SEQ = 2048
DEPTH = 1

MEM_LEN = 256
HEAD_DIM = 64
SB_HEADS = 8
RWKV_HEADS = 8
SB_WIDTH = SB_HEADS * HEAD_DIM
RWKV_WIDTH = RWKV_HEADS * HEAD_DIM
MIX_WIDTH = SB_WIDTH + RWKV_WIDTH
DECAY_LORA = 64
AAA_LORA = 64
GATE_LORA = 160
RWKV_IN = 3 * RWKV_WIDTH + DECAY_LORA + AAA_LORA + GATE_LORA
MIX_IN = 3 * SB_WIDTH + RWKV_IN
MEM_HEADS = 4
MEM_HEAD_DIM = 128
MEM_WIDTH = MEM_HEADS * MEM_HEAD_DIM
D_FF = 2816
SB_BLOCK = 128
NORM_EPS = 1e-6
LNX_EPS = 64e-5
RWKV_SPLITS = [RWKV_WIDTH, 2 * RWKV_WIDTH, 3 * RWKV_WIDTH,
               3 * RWKV_WIDTH + DECAY_LORA, 3 * RWKV_WIDTH + DECAY_LORA + AAA_LORA]

kernel_name = 'sb_rwkv7_macaron_sandwich_hybrid'


def rms_norm(x, g):
    xf = x.astype(jnp.float32)
    y = xf * lax.rsqrt(jnp.mean(xf * xf, axis=-1, keepdims=True) + NORM_EPS)
    return (y * g.astype(jnp.float32)).astype(x.dtype)


def swiglu(h, w_in, w_out):
    gate, up = jnp.split(h @ w_in, 2, axis=-1)
    return (jax.nn.silu(gate) * up) @ w_out


def token_shift(p):
    return jnp.pad(p, ((0, 0), (1, 0), (0, 0)))[:, :-1]


def stick_breaking_attention(q, k, v):
    S = q.shape[1]
    scale = q.shape[-1] ** -0.5
    outs = []
    for blk in range(S // SB_BLOCK):
        q0 = blk * SB_BLOCK
        q1 = q0 + SB_BLOCK
        z = jnp.einsum('bthd,bshd->bhts', q[:, q0:q1], k[:, :q1]).astype(jnp.float32) * scale
        causal = jnp.arange(q1)[None, :] < (q0 + jnp.arange(SB_BLOCK))[:, None]
        log_one_minus = jnp.where(causal, jax.nn.log_sigmoid(-z), 0.0)
        log_tail = lax.cumsum(log_one_minus, axis=3, reverse=True) - log_one_minus
        a = jnp.where(causal, jnp.exp(jax.nn.log_sigmoid(z) + log_tail), 0.0)
        outs.append(jnp.einsum('bhts,bshd->bthd', a.astype(v.dtype), v[:, :q1]))
    return jnp.concatenate(outs, axis=1)


def rwkv7_time_mix(p, mu, w0, w2, a0, a2, g2, k_k, k_a, r_k, lnx_g, lnx_b):
    B, S, _ = p.shape
    H, N = RWKV_HEADS, HEAD_DIM
    p = p + (token_shift(p) - p) * mu
    r, k, v, xw, xa, xg = jnp.split(p, RWKV_SPLITS, axis=-1)
    w = -jax.nn.softplus(-(w0 + jnp.tanh(xw) @ w2)) - 0.5
    decay = jnp.exp(-jnp.exp(w.astype(jnp.float32)))
    a = jax.nn.sigmoid(a0 + xa @ a2)
    g = jax.nn.sigmoid(xg) @ g2
    heads = lambda t: t.reshape(B, S, H, N)
    kk = heads(k * k_k).astype(jnp.float32)
    kk = kk / jnp.maximum(jnp.linalg.norm(kk, axis=-1, keepdims=True), 1e-12)
    k = k * (1.0 + (a - 1.0) * k_a)
    rh, kh, vh, ah = heads(r), heads(k), heads(v), heads(a)
    seq_first = lambda t: jnp.moveaxis(t.astype(jnp.float32), 1, 0)
    xs = (seq_first(rh), seq_first(heads(decay)), seq_first(kh), seq_first(vh),
          seq_first(-kk), seq_first(kk * ah))

    def step(state, inp):
        r_t, w_t, k_t, v_t, a_t, b_t = inp
        sa = jnp.einsum('bhvk,bhk->bhv', state, a_t)
        state = (state * w_t[:, :, None, :] + sa[..., :, None] * b_t[..., None, :]
                 + v_t[..., :, None] * k_t[..., None, :])
        return state, jnp.einsum('bhvk,bhk->bhv', state, r_t)

    state0 = jnp.zeros((B, H, N, N), jnp.float32)
    _, y = lax.scan(step, state0, xs)
    y = jnp.moveaxis(y, 0, 1)
    mean = jnp.mean(y, axis=-1, keepdims=True)
    var = jnp.mean(jnp.square(y - mean), axis=-1, keepdims=True)
    y = (y - mean) * lax.rsqrt(var + LNX_EPS)
    y = y * lnx_g.astype(jnp.float32).reshape(H, N) + lnx_b.astype(jnp.float32).reshape(H, N)
    bonus = jnp.sum((rh * kh * r_k).astype(jnp.float32), axis=-1, keepdims=True) * vh.astype(jnp.float32)
    return (y + bonus).reshape(B, S, RWKV_WIDTH).astype(p.dtype) * g


def memory_cross_attention(h, mem_n, w_q, w_kv, w_o):
    B, S, _ = h.shape
    M = mem_n.shape[1]
    q = (h @ w_q).reshape(B, S, MEM_HEADS, MEM_HEAD_DIM)
    km, vm = jnp.split(mem_n @ w_kv, 2, axis=-1)
    km = km.reshape(B, M, MEM_HEADS, MEM_HEAD_DIM)
    vm = vm.reshape(B, M, MEM_HEADS, MEM_HEAD_DIM)
    s = jnp.einsum('bthd,bmhd->bhtm', q, km).astype(jnp.float32) * (MEM_HEAD_DIM ** -0.5)
    pr = jax.nn.softmax(s, axis=-1)
    o = jnp.einsum('bhtm,bmhd->bthd', pr.astype(vm.dtype), vm).reshape(B, S, MEM_WIDTH)
    return o @ w_o


def setup_inputs(seed: int = 0) -> dict:
    key = jax.random.key(seed)
    ks = jax.random.split(key, 32)
    L = DEPTH

    def nrm(k, shape, scale):
        return jax.random.normal(k, shape, jnp.float32) * scale

    def gain(k, n):
        return 1.0 + nrm(k, (L, n), 0.02)

    pos = jnp.arange(RWKV_WIDTH, dtype=jnp.float32) / (RWKV_WIDTH - 1)
    w0_base = -6.5 + 5.0 * pos ** 0.85
    return {
        'x': nrm(ks[0], (BATCH, SEQ, D_MODEL), 1.0),
        'mem': nrm(ks[1], (BATCH, MEM_LEN, D_MODEL), 1.0),
        'ffn1_pre': gain(ks[2], D_MODEL),
        'ffn1_post': gain(ks[3], D_MODEL),
        'ffn1_w_in': nrm(ks[4], (L, D_MODEL, 2 * D_FF), D_MODEL ** -0.5),
        'ffn1_w_out': nrm(ks[5], (L, D_FF, D_MODEL), D_FF ** -0.5),
        'mix_pre': gain(ks[6], D_MODEL),
        'mix_post': gain(ks[7], D_MODEL),
        'mix_w_in': nrm(ks[8], (L, D_MODEL, MIX_IN), D_MODEL ** -0.5),
        'rwkv_mu': jax.random.uniform(ks[9], (L, RWKV_IN), jnp.float32),
        'rwkv_w0': w0_base[None, :] + nrm(ks[10], (L, RWKV_WIDTH), 0.01),
        'rwkv_w2': nrm(ks[11], (L, DECAY_LORA, RWKV_WIDTH), 0.1 * DECAY_LORA ** -0.5),
        'rwkv_a0': nrm(ks[12], (L, RWKV_WIDTH), 0.1),
        'rwkv_a2': nrm(ks[13], (L, AAA_LORA, RWKV_WIDTH), 0.1 * AAA_LORA ** -0.5),
        'rwkv_g2': nrm(ks[14], (L, GATE_LORA, RWKV_WIDTH), GATE_LORA ** -0.5),
        'rwkv_k_k': 0.85 + nrm(ks[15], (L, RWKV_WIDTH), 0.02),
        'rwkv_k_a': 1.0 + nrm(ks[16], (L, RWKV_WIDTH), 0.02),
        'rwkv_r_k': nrm(ks[17], (L, RWKV_HEADS, HEAD_DIM), 0.1),
        'rwkv_lnx_g': gain(ks[18], RWKV_WIDTH),
        'rwkv_lnx_b': nrm(ks[19], (L, RWKV_WIDTH), 0.02),
        'sb_out_g': gain(ks[20], SB_WIDTH),
        'mix_w_out': nrm(ks[21], (L, MIX_WIDTH, D_MODEL), MIX_WIDTH ** -0.5),
        'mem_pre': gain(ks[22], D_MODEL),
        'mem_post': gain(ks[23], D_MODEL),
        'mem_kv_g': gain(ks[24], D_MODEL),
        'mem_w_q': nrm(ks[25], (L, D_MODEL, MEM_WIDTH), D_MODEL ** -0.5),
        'mem_w_kv': nrm(ks[26], (L, D_MODEL, 2 * MEM_WIDTH), D_MODEL ** -0.5),
        'mem_w_o': nrm(ks[27], (L, MEM_WIDTH, D_MODEL), MEM_WIDTH ** -0.5),
        'ffn2_pre': gain(ks[28], D_MODEL),
        'ffn2_post': gain(ks[29], D_MODEL),
        'ffn2_w_in': nrm(ks[30], (L, D_MODEL, 2 * D_FF), D_MODEL ** -0.5),
        'ffn2_w_out': nrm(ks[31], (L, D_FF, D_MODEL), D_FF ** -0.5),
    }


def reference(x, mem, ffn1_pre, ffn1_post, ffn1_w_in, ffn1_w_out, mix_pre, mix_post, mix_w_in,
              rwkv_mu, rwkv_w0, rwkv_w2, rwkv_a0, rwkv_a2, rwkv_g2, rwkv_k_k, rwkv_k_a, rwkv_r_k,
              rwkv_lnx_g, rwkv_lnx_b, sb_out_g, mix_w_out, mem_pre, mem_post, mem_kv_g,
              mem_w_q, mem_w_kv, mem_w_o, ffn2_pre, ffn2_post, ffn2_w_in, ffn2_w_out):
    B, S, _ = x.shape
    h = x
    for l in range(DEPTH):
        h = h + 0.5 * rms_norm(swiglu(rms_norm(h, ffn1_pre[l]), ffn1_w_in[l], ffn1_w_out[l]), ffn1_post[l])

        u = rms_norm(h, mix_pre[l]) @ mix_w_in[l]
        sb_part, rw_part = jnp.split(u, [3 * SB_WIDTH], axis=-1)
        q, k, v = [t.reshape(B, S, SB_HEADS, HEAD_DIM) for t in jnp.split(sb_part, 3, axis=-1)]
        sb_o = stick_breaking_attention(q, k, v)
        sb_o = rms_norm(sb_o, sb_out_g[l].reshape(SB_HEADS, HEAD_DIM)).reshape(B, S, SB_WIDTH)
        rw_o = rwkv7_time_mix(rw_part, rwkv_mu[l], rwkv_w0[l], rwkv_w2[l], rwkv_a0[l], rwkv_a2[l],
                              rwkv_g2[l], rwkv_k_k[l], rwkv_k_a[l], rwkv_r_k[l],
                              rwkv_lnx_g[l], rwkv_lnx_b[l])
        mixed = jnp.concatenate([sb_o, rw_o], axis=-1) @ mix_w_out[l]
        h = h + rms_norm(mixed, mix_post[l])

        mem_n = rms_norm(mem, mem_kv_g[l])
        m_o = memory_cross_attention(rms_norm(h, mem_pre[l]), mem_n, mem_w_q[l], mem_w_kv[l], mem_w_o[l])
        h = h + rms_norm(m_o, mem_post[l])

        h = h + 0.5 * rms_norm(swiglu(rms_norm(h, ffn2_pre[l]), ffn2_w_in[l], ffn2_w_out[l]), ffn2_post[l])
    return h
```

```python
import contextlib
import math
import numpy as np
import concourse.bass as bass
import concourse.mybir as mybir
from concourse.bass_utils import run_bass_kernel_spmd

F32 = mybir.dt.float32
BF16 = mybir.dt.bfloat16
AF = mybir.ActivationFunctionType
ALU = mybir.AluOpType
AX = mybir.AxisListType

SAME_ENGINE_SYNC = True
ENGS = ("pe", "act", "dve", "pool", "sp")

NB = 2
SEQ = 2048
D = 1024
T = 512
NT = SEQ // T
DFF = 2816
NJ = DFF // 128
MEM = 256
CDEC = math.exp(-0.5)
NSLOT = 3
SLOT_ELEMS = 4096


class Tile:
    __slots__ = ("t", "name", "last_w", "readers", "excl")

    def __init__(self, t, name="", excl=False):
        self.t = t
        self.name = name
        self.last_w = None
        self.readers = []
        self.excl = excl

    def __getitem__(self, k):
        return self.t[k]


class _Op:
    __slots__ = ("eng", "fn", "deps", "waits", "signal", "sigval", "dma", "idx")


class Prog:
    def __init__(self, nc):
        self.nc = nc
        self.ops = {e: [] for e in ENGS}
        self.dma_sems = {}
        self.dma_tok = []
        self.final_tokens = []

    def _deps_for(self, reads, writes):
        deps = []
        for t in reads:
            if t.last_w is not None:
                deps.append(t.last_w)
            if t.excl:
                deps.extend(t.readers)
        for t in writes:
            if t.last_w is not None:
                deps.append(t.last_w)
            deps.extend(t.readers)
        return deps

    def _commit(self, tok, reads, writes):
        for t in reads:
            t.readers.append(tok)
        for t in writes:
            t.last_w = tok
            t.readers = []

    def _new(self, eng, fn, reads, writes):
        o = _Op()
        o.eng = eng
        o.fn = fn
        o.deps = self._deps_for(reads, writes)
        o.waits = []
        o.signal = False
        o.sigval = 0
        o.dma = None
        o.idx = len(self.ops[eng])
        self.ops[eng].append(o)
        return o

    mute = False

    def op(self, eng, fn, reads=(), writes=()):
        if self.mute:
            return None
        o = self._new(eng, fn, reads, writes)
        tok = ("e", eng, o.idx)
        self._commit(tok, reads, writes)
        return tok

    def dma(self, eng, fns, semkey, reads=(), writes=(), final=False):
        if self.mute:
            return None
        o = self._new(eng, fns, reads, writes)
        ent = self.dma_sems.setdefault(semkey, [len(self.dma_sems), 0])
        ent[1] += 16 * len(fns)
        did = len(self.dma_tok)
        self.dma_tok.append((semkey, ent[1]))
        o.dma = semkey
        tok = ("d", did)
        self._commit(tok, reads, writes)
        if final:
            self.final_tokens.append(tok)
        return tok

    def barrier(self, extra=()):
        engs = ("pe", "act", "dve", "pool")
        deps = list(extra)
        for Pn in engs:
            for o in reversed(self.ops[Pn]):
                if o.fn is not None and o.dma is None:
                    deps.append(("e", Pn, o.idx))
                    break
        for E in engs:
            o = self._new(E, None, (), ())
            o.deps = list(deps)

    def emit(self, final_eng="sp"):
        nc = self.nc
        fo = self._new(final_eng, None, (), ())
        fo.deps = list(self.final_tokens)
        for E in ENGS:
            waited = {}
            for o in self.ops[E]:
                for tok in o.deps:
                    if tok[0] == "e":
                        _, Pn, idx = tok
                        if Pn == E and (not SAME_ENGINE_SYNC or idx >= o.idx):
                            continue
                        key = ("e", Pn)
                        if waited.get(key, -1) >= idx:
                            continue
                        waited[key] = idx
                        self.ops[Pn][idx].signal = True
                        o.waits.append(tok)
                    else:
                        semkey, val = self.dma_tok[tok[1]]
                        key = ("d", semkey)
                        if waited.get(key, -1) >= val:
                            continue
                        waited[key] = val
                        o.waits.append(tok)
        for E in ENGS:
            c = 0
            for o in self.ops[E]:
                if o.signal:
                    c += 1
                o.sigval = c
        with contextlib.ExitStack() as st:
            esem = {E: st.enter_context(nc.semaphore("s_" + E)) for E in ENGS}
            dsem = {}
            for k, (i, _) in self.dma_sems.items():
                dsem[k] = st.enter_context(nc.semaphore("d%d" % i))
            block = st.enter_context(nc.Block())

            def run(E, eng):
                for o in self.ops[E]:
                    for tok in o.waits:
                        if tok[0] == "e":
                            _, Pn, idx = tok
                            eng.wait_ge(esem[Pn], self.ops[Pn][idx].sigval)
                        else:
                            semkey, val = self.dma_tok[tok[1]]
                            eng.wait_ge(dsem[semkey], val)
                    if o.fn is None:
                        continue
                    if o.dma is not None:
                        for f in o.fn:
                            f(eng).then_inc(dsem[o.dma], 16)
                    else:
                        ins = o.fn(eng)
                        if o.signal:
                            ins.then_inc(esem[E], 1)

            @block.tensor
            def _(eng):
                run("pe", eng)

            @block.scalar
            def _(eng):
                run("act", eng)

            @block.vector
            def _(eng):
                run("dve", eng)

            @block.gpsimd
            def _(eng):
                run("pool", eng)

            @block.sync
            def _(eng):
                run("sp", eng)


RW0 = 1536


def _piece(W, cols):
    K = W.shape[0]
    kc = K // 128
    sub = W[:, cols]
    return np.ascontiguousarray(sub.reshape(kc, 128, len(cols)).transpose(1, 0, 2))


def _plan():
    pl = []
    for f in ("f1", "f2"):
        for i in range(NJ // 2):
            pl.append((f + "_in%d" % i, 8, 512))
        for c in range(8):
            pl.append((f + "_out%d" % c, NJ, 128))
    pl += [("mix_q", 8, 512), ("mix_k", 8, 512), ("mix_v", 8, 512), ("mix_lora", 8, 288)]
    for fc in range(4):
        pl.append(("mix_rw%d" % fc, 8, 384))
    pl += [("mixo0", 8, 512), ("mixo1", 8, 512), ("memq", 8, 512), ("memo", 4, 1024),
           ("memk", 8, 512), ("memv", 8, 512)]
    return pl


def _offsets():
    off = {}
    o = 0
    for name, kc, gc in _plan():
        off[name] = (o, kc, gc)
        o += kc * gc
    return off, o


def _host_weights(inp):
    ar = np.arange
    pieces = {}
    for f, wi, wo in (("f1", "ffn1_w_in", "ffn1_w_out"), ("f2", "ffn2_w_in", "ffn2_w_out")):
        Wi = inp[wi][0]
        Wo = inp[wo][0]
        for i in range(NJ // 2):
            cols = np.concatenate([ar(128) + (2 * i) * 128, ar(128) + DFF + (2 * i) * 128,
                                   ar(128) + (2 * i + 1) * 128, ar(128) + DFF + (2 * i + 1) * 128])
            pieces[f + "_in%d" % i] = _piece(Wi, cols)
        for c in range(8):
            pieces[f + "_out%d" % c] = _piece(Wo, ar(128) + c * 128)
    Wm = inp["mix_w_in"][0]
    pieces["mix_q"] = _piece(Wm, ar(512))
    pieces["mix_k"] = _piece(Wm, ar(512) + 512)
    pieces["mix_v"] = _piece(Wm, ar(512) + 1024)
    pieces["mix_lora"] = _piece(Wm, ar(288) + RW0 + 1536)
    for fc in range(4):
        cols = np.concatenate([ar(128) + RW0 + fc * 128, ar(128) + RW0 + 512 + fc * 128,
                               ar(128) + RW0 + 1024 + fc * 128])
        pieces["mix_rw%d" % fc] = _piece(Wm, cols)
    Wo = inp["mix_w_out"][0]
    pieces["mixo0"] = _piece(Wo, ar(512))
    pieces["mixo1"] = _piece(Wo, ar(512) + 512)
    pieces["memq"] = _piece(inp["mem_w_q"][0], ar(512))
    pieces["memo"] = _piece(inp["mem_w_o"][0], ar(1024))
    pieces["memk"] = _piece(inp["mem_w_kv"][0], ar(512))
    pieces["memv"] = _piece(inp["mem_w_kv"][0], ar(512) + 512)
    off, total = _offsets()
    wf = np.empty((128, total), np.float32)
    for name, (o, kc, gc) in off.items():
        wf[:, o:o + kc * gc] = pieces[name].reshape(128, kc * gc)
    return wf


V_F1PRE, V_F1POST, V_MIXPRE, V_MIXPOST, V_MEMPRE, V_MEMPOST, V_F2PRE, V_F2POST = [8 * i for i in range(8)]
V_MU_R, V_MU_K, V_MU_V = 64, 68, 72
V_MU_XWA, V_MU_XG0, V_MU_XG1 = 76, 77, 78
V_W0, V_A0, V_KK, V_KA, V_RK, V_SBG = 79, 83, 87, 91, 95, 99
V_MEMKV = 103
NV = 112


def _host_vecs(inp):
    v = np.zeros((128, NV), np.float32)

    def put(col, vec):
        n = vec.shape[0]
        nch = (n + 127) // 128
        for c in range(nch):
            seg = vec[c * 128:(c + 1) * 128]
            v[:seg.shape[0], col + c] = seg

    put(V_F1PRE, inp["ffn1_pre"][0]); put(V_F1POST, inp["ffn1_post"][0])
    put(V_MIXPRE, inp["mix_pre"][0]); put(V_MIXPOST, inp["mix_post"][0])
    put(V_MEMPRE, inp["mem_pre"][0]); put(V_MEMPOST, inp["mem_post"][0])
    put(V_F2PRE, inp["ffn2_pre"][0]); put(V_F2POST, inp["ffn2_post"][0])
    mu = inp["rwkv_mu"][0]
    put(V_MU_R, mu[0:512]); put(V_MU_K, mu[512:1024]); put(V_MU_V, mu[1024:1536])
    put(V_MU_XWA, mu[1536:1664]); put(V_MU_XG0, mu[1664:1792]); put(V_MU_XG1, mu[1792:1824])
    put(V_W0, inp["rwkv_w0"][0]); put(V_A0, inp["rwkv_a0"][0])
    put(V_KK, inp["rwkv_k_k"][0]); put(V_KA, inp["rwkv_k_a"][0])
    put(V_RK, inp["rwkv_r_k"][0].reshape(-1)); put(V_SBG, inp["sb_out_g"][0])
    put(V_MEMKV, inp["mem_kv_g"][0])
    return v


def _host_small(inp):
    s = np.zeros((128, 3, 512), np.float32)
    s[0:64, 0] = inp["rwkv_w2"][0]
    s[64:128, 0] = inp["rwkv_a2"][0]
    g2 = inp["rwkv_g2"][0]
    s[:, 1] = g2[0:128]
    s[0:32, 2] = g2[128:160]
    return s


def _host_bvecs(inp):
    b = np.empty((128, 1024), np.float32)
    b[:, 0:512] = inp["rwkv_lnx_g"][0][None, :]
    b[:, 512:1024] = inp["rwkv_lnx_b"][0][None, :]
    return b


def build_nc(ntiles_limit=None, dbg=False, stages=99, rwsub=99):
    nc = bass.Bass("TRN2", target_bir_lowering=False)
    off, WTOT = _offsets()
    x_d = nc.dram_tensor("x", [NB, SEQ, D], F32, kind="ExternalInput").ap()
    mem_d = nc.dram_tensor("mem", [NB, MEM, D], F32, kind="ExternalInput").ap()
    wf_d = nc.dram_tensor("wf", [128, WTOT], F32, kind="ExternalInput").ap()
    vecs_d = nc.dram_tensor("vecs", [128, NV], F32, kind="ExternalInput").ap()
    small_d = nc.dram_tensor("small", [128, 3, 512], F32, kind="ExternalInput").ap()
    bvecs_d = nc.dram_tensor("bvecs", [128, 1024], F32, kind="ExternalInput").ap()
    out_d = nc.dram_tensor("out", [NB, SEQ, D], F32, kind="ExternalOutput").ap()
    wb_d = nc.dram_tensor("wbf", [128, WTOT], BF16, kind="Internal").ap()
    dbg_d = {}
    if dbg:
        for nm in ("h1", "q", "sbo", "rwo", "h2", "h3"):
            dbg_d[nm] = nc.dram_tensor("dbg_" + nm, [128, 8, T], F32, kind="ExternalOutput").ap()

    P = Prog(nc)
    with contextlib.ExitStack() as st:
        def sb(name, shape, dt=F32):
            return st.enter_context(nc.sbuf_tensor("sb_" + name, shape, dt))

        def sbt(name, shape, dt=F32):
            return Tile(sb(name, shape, dt), name)

        h_t = sb("h", [128, 8, T])
        hT = [Tile(h_t, "h%d" % c) for c in range(8)]
        xn_t = sb("xn", [128, 8, T], BF16)
        xnT = [Tile(xn_t, "xn%d" % c) for c in range(8)]
        tmp_t = sb("tmp8", [128, 8 * T])
        tmpT = [Tile(tmp_t, "tmp%d" % c) for c in range(8)]
        tmp8 = tmp_t[:].rearrange("p (c t) -> p c t", t=T)
        ost = tmp_t[:].rearrange("p (n f) -> p n f", f=D)
        stg = sbt("stg", [128, 2, D])
        wslot = [sbt("wslot%d" % i, [128, SLOT_ELEMS], BF16) for i in range(NSLOT)]
        KTt = sb("KT", [128, 4, SEQ], BF16)
        KT = [Tile(KTt, "KT%d" % c) for c in range(4)]
        Vt = sb("V", [128, 16, 512], BF16)
        VT = [Tile(Vt, "V%d" % c) for c in range(16)]
        vecs = sbt("vecs", [128, NV])
        bvecs = sbt("bvecs", [128, 1024])
        smallw = sbt("smallw", [128, 3, 512], BF16)
        kmT = sbt("kmT", [128, 4, MEM], BF16)
        vmT = sbt("vm", [128, 2, 512], BF16)
        S32 = sbt("S32", [128, 4, 128])
        S16 = sbt("S16", [128, 4, 128], BF16)
        carry = sbt("carry", [128, 16])
        rstd = sbt("rstd", [128, T])
        sqb = [sbt("sqb%d" % i, [128, T], BF16) for i in range(2)]
        ident = sbt("ident", [128, 128])
        identb = sbt("identb", [128, 128], BF16)
        onesf = sbt("onesf", [128, 512])
        ones_bf = sbt("ones_bf", [128, 128], BF16)
        blk_bf = sbt("blk_bf", [128, 128], BF16)
        triU_f = sbt("triU_f", [128, 128])
        triU_bf = sbt("triU_bf", [128, 128], BF16)
        mask4 = sbt("mask4", [128, 4, 512], BF16)
        maskAB = sbt("maskAB", [128, 512])
        maskN2 = sbt("maskN2", [128, 256])
        resetm = sbt("resetm", [128, 512])
        ind2f = sbt("ind2f", [128, 2])
        ind2 = sbt("ind2", [128, 2], BF16)
        epsA = sbt("epsA", [128, 4])

        ARENA_BYTES = 72 * 1024
        arena = sb("arena", [128, ARENA_BYTES // 4])
        apos = [0]

        def aset(off):
            apos[0] = off

        def ar(shape, dt=F32):
            n = 1
            for d_ in shape[1:]:
                n *= d_
            nbytes = n * (2 if dt == BF16 else 4)
            nbytes = (nbytes + 31) // 32 * 32
            a0 = apos[0]
            apos[0] += nbytes
            assert apos[0] <= ARENA_BYTES, ("arena overflow", apos[0])
            v = arena[:, a0 // 4:(a0 + nbytes) // 4]
            if dt == BF16:
                v = v.bitcast(BF16)
            v = v[:, 0:n]
            if len(shape) == 3:
                v = v.rearrange("p (a b) -> p a b", b=shape[2])
            if shape[0] < 128:
                v = v[0:shape[0]]
            return v

        def art(name, shape, dt=F32):
            return Tile(ar(shape, dt), name)

        aset(0)
        mixcat_t = ar([128, 8, T], BF16)
        mixcat = [Tile(mixcat_t, "mixcat%d" % c) for c in range(8)]
        MIX0 = apos[0]
        aset(0)
        hid_t = ar([128, NJ, T], BF16)
        hidT = [Tile(hid_t, "hid%d" % j) for j in range(NJ)]
        sgt = [art("sg%d" % i, [128, T]) for i in range(2)]
        aset(MIX0)
        QTt = ar([128, 4, T], BF16)
        QT = [Tile(QTt, "QT%d" % c) for c in range(4)]
        Ebuf = [art("E%d" % i, [128, T]) for i in range(2)]
        SPb = [art("SPb%d" % i, [128, T], BF16) for i in range(2)]
        t1b = [art("t1_%d" % i, [128, T]) for i in range(2)]
        Ab = [art("A%d" % i, [128, T], BF16) for i in range(2)]
        accb = [art("acc%d" % i, [128, T]) for i in range(2)]
        Oraw_t = ar([128, 4, T])
        Oraw = [Tile(Oraw_t, "Oraw%d" % c) for c in range(4)]
        aset(MIX0)
        rawb = [art("raw%d" % i, [128, T + 8]) for i in range(2)]
        dtmp = art("dtmp", [128, T])
        txa = art("txa", [128, T], BF16)
        sxg = art("sxg", [128, 2, T], BF16)
        r32 = art("r32", [128, T]); k32 = art("k32", [128, T])
        vTb = art("vTb", [128, T], BF16)
        sig = art("sig", [128, T]); cum = art("cum", [128, T])
        eW = art("eW", [128, T]); eWx = art("eWx", [128, T]); eWi = art("eWi", [128, T])
        WC = art("WC", [128, 8])
        a32 = art("a32", [128, T]); kk32 = art("kk32", [128, T]); kkn = art("kkn", [128, T])
        kmod = art("kmod", [128, T])
        ARt = art("AR", [128, 4, 256], BF16)
        BTt = art("BT", [128, T], BF16); KTl = art("KTl", [128, T], BF16)
        rkr_t = ar([128, 4, T], BF16)
        rkr = [Tile(rkr_t, "rkr%d" % c) for c in range(4)]
        Btok = art("Btok", [128, 4, 128], BF16); Ktok = art("Ktok", [128, 4, 128], BF16)
        Vtok_t = ar([128, 4, 512], BF16)
        Vtok = [Tile(Vtok_t, "Vtok%d" % c) for c in range(4)]
        MMb = art("MMb", [128, 2, 256], BF16); KMb = art("KMb", [128, 2, 256], BF16)
        Xb = [art("X%d" % i, [128, 2, 128], BF16) for i in range(2)]
        XTb = [art("XT%d" % i, [128, 2, 128], BF16) for i in range(2)]
        SA32 = art("SA32", [128, 128]); SA16 = art("SA16", [128, 128], BF16)
        Yall_t = ar([128, 4, 512])
        Yall = [Tile(Yall_t, "Yall%d" % c) for c in range(4)]
        st8 = art("st8", [128, 6, 8])
        rwo = art("rwo", [128, 512], BF16)
        cumx = sig
        xwa32 = r32
        ysq = sig
        yn = cum
        print("arena rwkv end", apos[0])
        nrm = dtmp
        b32 = kk32
        aset(0)
        qTm_t = ar([128, 4, T], BF16)
        qTm = [Tile(qTm_t, "qTm%d" % c) for c in range(4)]
        Em = [art("Em%d" % i, [128, T], BF16) for i in range(4)]
        oTm_t = ar([128, 4, T], BF16)
        oTm = [Tile(oTm_t, "oTm%d" % c) for c in range(4)]
        rsb = art("rsb", [128, T])
        memn_t = ar([128, 8, MEM], BF16)
        memn = [Tile(memn_t, "memn%d" % c) for c in range(8)]

        banks = [Tile(st.enter_context(nc.psum_tensor("pb%d" % i, [128, 512], F32)), "pb%d" % i, excl=True)
                 for i in range(8)]
        rot = [0]

        def nb():
            b = banks[rot[0] % 6]
            rot[0] += 1
            return b

        def MM(out, lhsT, rhs, start, stop, reads, writes):
            P.op("pe", lambda e: e.matmul(out, lhsT=lhsT, rhs=rhs, start=start, stop=stop), reads, writes)

        def TR(out, in_, idn, reads, writes):
            P.op("pe", lambda e: e.transpose(out, in_, idn), reads, writes)

        def ACT(out, in_, func, reads, writes, bias=None, scale=1.0):
            if bias is None:
                P.op("act", lambda e: e.activation(out=out, in_=in_, func=func, scale=scale), reads, writes)
            else:
                P.op("act", lambda e: e.activation(out=out, in_=in_, func=func, bias=bias, scale=scale), reads, writes)

        def CP(eng, out, in_, reads, writes):
            if eng == "act":
                P.op("act", lambda e: e.copy(out=out, in_=in_), reads, writes)
            else:
                P.op(eng, lambda e: e.tensor_copy(out=out, in_=in_), reads, writes)

        def TT(eng, out, in0, in1, op, reads, writes):
            P.op(eng, lambda e: e.tensor_tensor(out=out, in0=in0, in1=in1, op=op), reads, writes)

        def STT(eng, out, in0, scalar, in1, op0, op1, reads, writes):
            P.op("dve", lambda e: e.scalar_tensor_tensor(out=out, in0=in0, scalar=scalar, in1=in1, op0=op0, op1=op1),
                 reads, writes)

        def TS(eng, out, in0, s1, s2, op0, op1, reads, writes):
            P.op(eng, lambda e: e.tensor_scalar(out=out, in0=in0, scalar1=s1, scalar2=s2, op0=op0, op1=op1),
                 reads, writes)

        def TS1(eng, out, in0, s1, op0, reads, writes):
            P.op(eng, lambda e: e.tensor_single_scalar(out=out, in_=in0, scalar=s1, op=op0), reads, writes)

        def MSET(eng, out, val, writes):
            P.op(eng, lambda e: e.memset(out, val), (), writes)

        def RECIP(out, in_, reads, writes):
            P.op("dve", lambda e: e.reciprocal(out=out, in_=in_), reads, writes)

        def ASEL(out, in_, pattern, cmp, base, cm, reads, writes):
            P.op("pool", lambda e: e.affine_select(out=out, in_=in_, pattern=pattern, compare_op=cmp, fill=0.0,
                                                   base=base, channel_multiplier=cm), reads, writes)

        def vcol(c):
            return vecs[:, c:c + 1]

        MSET("pool", onesf[:], 1.0, [onesf])
        ASEL(ident[:], onesf[:, 0:128], [[-1, 128]], ALU.is_equal, 0, 1, [onesf], [ident])
        CP("dve", identb[:], ident[:], [ident], [identb])
        CP("dve", ones_bf[:], onesf[:, 0:128], [onesf], [ones_bf])
        MSET("pool", blk_bf[:], 0.0, [blk_bf])
        MSET("pool", blk_bf[0:64, 0:64], 1.0, [blk_bf])
        MSET("pool", blk_bf[64:128, 64:128], 1.0, [blk_bf])
        ASEL(triU_f[:], onesf[:, 0:128], [[-1, 128]], ALU.is_gt, 0, 1, [onesf], [triU_f])
        CP("dve", triU_bf[:], triU_f[:], [triU_f], [triU_bf])
        for r in range(4):
            ASEL(rstd[:], onesf[:], [[1, 512]], ALU.is_gt, -128 * r, -1, [onesf], [rstd])
            CP("dve", mask4[:, r, :], rstd[:], [rstd], [mask4])
        for hh in range(2):
            ASEL(maskAB[:, hh * 256:hh * 256 + 128], onesf[:, 0:128], [[1, 128]], ALU.is_gt, 0, -1, [onesf], [maskAB])
            ASEL(maskAB[:, hh * 256 + 128:hh * 256 + 256], onesf[:, 0:128], [[1, 128]], ALU.is_ge, 0, -1, [onesf], [maskAB])
            CP("pool", maskN2[:, hh * 128:(hh + 1) * 128], triU_f[:], [triU_f], [maskN2])
        MSET("pool", resetm[:], 1.0, [resetm])
        for c in range(4):
            MSET("pool", resetm[:, c * 128:c * 128 + 1], 0.0, [resetm])
        MSET("pool", ind2f[:], 0.0, [ind2f])
        MSET("pool", ind2f[0:64, 0:1], 1.0, [ind2f])
        MSET("pool", ind2f[64:128, 1:2], 1.0, [ind2f])
        CP("dve", ind2[:], ind2f[:], [ind2f], [ind2])
        MSET("pool", epsA[:, 0:1], 1e-6, [epsA])
        MSET("pool", epsA[:, 1:2], 4e-6, [epsA])
        MSET("pool", epsA[:, 2:3], 64e-5, [epsA])
        MSET("pool", epsA[:, 3:4], 1.0, [epsA])
        P.dma("sp", [lambda e: e.dma_start(out=vecs[:], in_=vecs_d)], "vecs", writes=[vecs])
        P.dma("sp", [lambda e: e.dma_start(out=bvecs[:], in_=bvecs_d)], "bvecs", writes=[bvecs])
        P.dma("pool", [lambda e: e.dma_start(out=smallw[:], in_=small_d)], "smallw", writes=[smallw])

        wpiece = {}
        for name, kc, gc in _plan():
            o, _, _ = off[name]
            n = kc * gc
            tl = Tile(None, "w_" + name)
            wpiece[name] = tl
            P.dma("pool", [lambda e, o=o, n=n: e.dma_start(out=wb_d[:, o:o + n], in_=wf_d[:, o:o + n])],
                  ("wc", name), writes=[tl])

        wrot = [0]

        def wload(name):
            o, kc, gc = off[name]
            n = kc * gc
            s = wslot[wrot[0] % NSLOT]
            key = ("ws", wrot[0] % NSLOT)
            wrot[0] += 1
            P.dma("sp", [lambda e: e.dma_start(out=s[:, 0:n], in_=wb_d[:, o:o + n])], key,
                  reads=[wpiece[name]], writes=[s])
            return s, s[:, 0:n].rearrange("p (k c) -> p k c", c=gc)

        sqrot = [0]

        def sumsq_bcast(srcs, src_tiles, n, ones_l, nfree=T):
            pbk = nb()
            for c in range(n):
                q = sqb[sqrot[0] % 2]
                sqrot[0] += 1
                ACT(q[:, 0:nfree], srcs[c], AF.Square, [src_tiles[c]], [q])
                MM(pbk[:, 0:nfree], ones_l[:], q[:, 0:nfree], c == 0, c == n - 1, [q, ones_l], [pbk])
            return pbk

        def rmsnorm_to_xn(gcol):
            pbk = sumsq_bcast([h_t[:, c, :] for c in range(8)], hT, 8, ones_bf)
            ACT(rstd[:], pbk[:], AF.Sqrt, [pbk, epsA], [rstd], bias=epsA[:, 0:1], scale=1.0 / D)
            RECIP(rstd[:], rstd[:], [rstd], [rstd])
            for c in range(8):
                STT("dve", xn_t[:, c, :], h_t[:, c, :], vcol(gcol + c), rstd[:], ALU.mult, ALU.mult,
                    [hT[c], rstd, vecs], [xnT[c]])

        def post_residual(gcol, half):
            pbk = sumsq_bcast([tmp8[:, c, :] for c in range(8)], tmpT, 8, ones_bf)
            if half:
                ACT(rstd[:], pbk[:], AF.Sqrt, [pbk, epsA], [rstd], bias=epsA[:, 1:2], scale=4.0 / D)
            else:
                ACT(rstd[:], pbk[:], AF.Sqrt, [pbk, epsA], [rstd], bias=epsA[:, 0:1], scale=1.0 / D)
            RECIP(rstd[:], rstd[:], [rstd], [rstd])
            for c in range(8):
                STT("dve", tmp8[:, c, :], tmp8[:, c, :], vcol(gcol + c), rstd[:], ALU.mult, ALU.mult,
                    [tmpT[c], rstd, vecs], [tmpT[c]])
                TT("pool", h_t[:, c, :], h_t[:, c, :], tmp8[:, c, :], ALU.add, [hT[c], tmpT[c]], [hT[c]])

        def ffn(f, pre, post):
            BAR()
            rmsnorm_to_xn(pre)
            for i in range(NJ // 2):
                s, w = wload(f + "_in%d" % i)
                for jj in range(2):
                    j = 2 * i + jj
                    pg = nb()
                    for k in range(8):
                        MM(pg[:], w[:, k, jj * 256:jj * 256 + 128], xn_t[:, k, :], k == 0, k == 7, [s, xnT[k]], [pg])
                    pu = nb()
                    for k in range(8):
                        MM(pu[:], w[:, k, jj * 256 + 128:jj * 256 + 256], xn_t[:, k, :], k == 0, k == 7,
                           [s, xnT[k]], [pu])
                    sg = sgt[j % 2]
                    ACT(sg[:], pg[:], AF.Silu, [pg], [sg])
                    TT("dve", hid_t[:, j, :], pu[:], sg[:], ALU.mult, [pu, sg], [hidT[j]])
            for c in range(8):
                s, w = wload(f + "_out%d" % c)
                po = nb()
                for k in range(NJ):
                    MM(po[:], w[:, k, :], hid_t[:, k, :], k == 0, k == NJ - 1, [s, hidT[k]], [po])
                CP("act", tmp8[:, c, :], po[:], [po], [tmpT[c]])
            post_residual(post, True)

        arena_dma = []

        def BAR():
            P.barrier(extra=arena_dma)

        def dump(nm, ap3, tiles):
            if dbg and nm in dbg_d:
                tok = P.dma("pool", [lambda e: e.dma_start(out=dbg_d[nm][:, 0:ap3.shape[1], :], in_=ap3)], ("dbg", nm),
                            reads=tiles, final=True)
                arena_dma.append(tok)

        tile_count = 0
        for b in range(NB):
            if stages >= 4:
                BAR()
                P.dma("sp", [lambda e, b=b: e.dma_start(out=stg[:], in_=mem_d[b].rearrange("(n p) f -> p n f", p=128))],
                      "stg", writes=[stg])
                memT32 = tmp_t[:, 0:8 * MEM].rearrange("p (c t) -> p c t", t=MEM)
                for n in range(2):
                    for g in range(2):
                        pbk = nb()
                        for cc in range(4):
                            c = g * 4 + cc
                            TR(pbk[:, cc * 128:(cc + 1) * 128], stg[:, n, c * 128:(c + 1) * 128], ident[:],
                               [stg, ident], [pbk])
                        CP("act", memT32[:, g * 4:g * 4 + 4, n * 128:(n + 1) * 128],
                           pbk[:].rearrange("p (a t) -> p a t", a=4), [pbk], [tmpT[0], tmpT[1], tmpT[2], tmpT[3]])
                mt = [tmpT[0], tmpT[1], tmpT[2], tmpT[3]]
                pbk = nb()
                for c in range(8):
                    q = sqb[sqrot[0] % 2]
                    sqrot[0] += 1
                    ACT(q[:, 0:MEM], memT32[:, c, :], AF.Square, mt, [q])
                    MM(pbk[:, 0:MEM], ones_bf[:], q[:, 0:MEM], c == 0, c == 7, [q, ones_bf], [pbk])
                ACT(rstd[:, 0:MEM], pbk[:, 0:MEM], AF.Sqrt, [pbk, epsA], [rstd], bias=epsA[:, 0:1], scale=1.0 / D)
                RECIP(rstd[:, 0:MEM], rstd[:, 0:MEM], [rstd], [rstd])
                for c in range(8):
                    STT("dve", memn_t[:, c, :], memT32[:, c, :], vcol(V_MEMKV + c), rstd[:, 0:MEM], ALU.mult, ALU.mult,
                        mt + [rstd, vecs], [memn[c]])
                s, w = wload("memk")
                for c in range(4):
                    pbk = nb()
                    for k in range(8):
                        MM(pbk[:, 0:MEM], w[:, k, c * 128:(c + 1) * 128], memn_t[:, k, :], k == 0, k == 7,
                           [s, memn[k]], [pbk])
                    CP("act", kmT[:, c, :], pbk[:, 0:MEM], [pbk], [kmT])
                s, w = wload("memv")
                for mb in range(2):
                    pbk = nb()
                    for k in range(8):
                        MM(pbk[:], memn_t[:, k, mb * 128:(mb + 1) * 128], w[:, k, :], k == 0, k == 7,
                           [s, memn[k]], [pbk])
                    CP("act", vmT[:, mb, :], pbk[:], [pbk], [vmT])
            MSET("pool", S32[:], 0.0, [S32])
            MSET("pool", S16[:], 0.0, [S16])

            for it in range(NT):
                if ntiles_limit is not None and tile_count >= ntiles_limit:
                    break
                tile_count += 1
                t0 = it * T
                qb0 = it * 4
                first = (it == 0)
                dd = dbg and tile_count == 1
                for hf in range(2):
                    P.dma("sp", [lambda e, b=b, hf=hf, t0=t0: e.dma_start(
                        out=stg[:], in_=x_d[b, t0 + hf * 256:t0 + hf * 256 + 256, :].rearrange("(n p) f -> p n f", p=128))],
                        "stg", writes=[stg])
                    for n in range(2):
                        tb = hf * 2 + n
                        for g in range(2):
                            pbk = nb()
                            for cc in range(4):
                                c = g * 4 + cc
                                TR(pbk[:, cc * 128:(cc + 1) * 128], stg[:, n, c * 128:(c + 1) * 128], ident[:],
                                   [stg, ident], [pbk])
                            CP("act" if g == 0 else "dve", h_t[:, g * 4:g * 4 + 4, tb * 128:(tb + 1) * 128],
                               pbk[:].rearrange("p (a t) -> p a t", a=4), [pbk], hT[g * 4:g * 4 + 4])
                if stages >= 1:
                    ffn("f1", V_F1PRE, V_F1POST)
                if dd:
                    dump("h1", h_t[:], hT)
                if stages >= 2:
                    BAR()
                    rmsnorm_to_xn(V_MIXPRE)
                    s, w = wload("mix_q")
                    for c in range(4):
                        pbk = nb()
                        for k in range(8):
                            MM(pbk[:], w[:, k, c * 128:(c + 1) * 128], xn_t[:, k, :], k == 0, k == 7, [s, xnT[k]], [pbk])
                        P.op("act", lambda e, c=c, pbk=pbk: e.mul(out=QTt[:, c, :], in_=pbk[:], mul=0.125), [pbk], [QT[c]])
                    s, w = wload("mix_k")
                    for c in range(4):
                        pbk = nb()
                        for k in range(8):
                            MM(pbk[:], w[:, k, c * 128:(c + 1) * 128], xn_t[:, k, :], k == 0, k == 7, [s, xnT[k]], [pbk])
                        CP("dve", KTt[:, c, t0:t0 + T], pbk[:], [pbk], [KT[c]])
                    s, w = wload("mix_v")
                    for tb in range(4):
                        pbk = nb()
                        for k in range(8):
                            MM(pbk[:], xn_t[:, k, tb * 128:(tb + 1) * 128], w[:, k, :], k == 0, k == 7, [s, xnT[k]], [pbk])
                        CP("act", Vt[:, qb0 + tb, :], pbk[:], [pbk], [VT[qb0 + tb]])
                    for hd in range(8):
                        fc = hd // 2
                        po = (hd % 2) * 64
                        acc = accb[hd % 2]
                        Oacc = banks[6 + hd % 2]
                        MSET("pool", acc[:], 0.0, [acc])
                        kbs = list(range(qb0 + 3, -1, -1))
                        for idx, kb in enumerate(kbs):
                            E = Ebuf[idx % 2]; SP = SPb[idx % 2]; t1 = t1b[idx % 2]; A = Ab[idx % 2]
                            pz = nb()
                            MM(pz[:], KTt[po:po + 64, fc, kb * 128:(kb + 1) * 128], QTt[po:po + 64, fc, :], True, True,
                               [KT[fc], QT[fc]], [pz])
                            ACT(E[:], pz[:], AF.Exp, [pz], [E])
                            ACT(SP[:], E[:], AF.Ln, [E, epsA], [SP], bias=epsA[:, 3:4])
                            diag = kb >= qb0
                            if diag:
                                r = kb - qb0
                                TT("pool", SP[:], SP[:], mask4[:, r, :], ALU.mult, [SP, mask4], [SP])
                            pw = nb()
                            MM(pw[:], triU_bf[:], SP[:], True, True, [triU_bf, SP], [pw])
                            pc = nb()
                            MM(pc[:], ones_bf[:], SP[:], True, True, [ones_bf, SP], [pc])
                            TT("dve", t1[:], pz[:], SP[:], ALU.subtract, [pz, SP], [t1])
                            TT("dve", t1[:], t1[:], pw[:], ALU.subtract, [t1, pw], [t1])
                            TT("pool", t1[:], t1[:], acc[:], ALU.subtract, [t1, acc], [t1])
                            TT("dve", acc[:], acc[:], pc[:], ALU.add, [acc, pc], [acc])
                            ACT(A[:], t1[:], AF.Exp, [t1], [A])
                            if diag:
                                TT("pool", A[:], A[:], mask4[:, r, :], ALU.mult, [A, mask4], [A])
                            MM(Oacc[:], Vt[:, kb, fc * 128:(fc + 1) * 128], A[:], idx == 0, idx == len(kbs) - 1,
                               [VT[kb], A], [Oacc])
                        CP("act", Oraw_t[po:po + 64, fc, :], Oacc[po:po + 64, :], [Oacc], [Oraw[fc]])
                    for fc in range(4):
                        q = sqb[sqrot[0] % 2]
                        sqrot[0] += 1
                        ACT(q[:], Oraw_t[:, fc, :], AF.Square, [Oraw[fc]], [q])
                        pbk = nb()
                        MM(pbk[:], blk_bf[:], q[:], True, True, [blk_bf, q], [pbk])
                        ACT(rstd[:], pbk[:], AF.Sqrt, [pbk, epsA], [rstd], bias=epsA[:, 0:1], scale=1.0 / 64)
                        RECIP(rstd[:], rstd[:], [rstd], [rstd])
                        STT("dve", mixcat_t[:, fc, :], Oraw_t[:, fc, :], vcol(V_SBG + fc), rstd[:], ALU.mult, ALU.mult,
                            [Oraw[fc], rstd, vecs], [mixcat[fc]])

                    P.mute = stages < 3
                    BAR()
                    rawi = [0]

                    def lerp(ps_ap, ps_tile, M, ci, mucol, dest, dest_tile):
                        raw = rawb[rawi[0] % 2]
                        rawi[0] += 1
                        if first:
                            MSET("pool", raw[0:M, 0:1], 0.0, [raw])
                        else:
                            CP("pool", raw[0:M, 0:1], carry[0:M, ci:ci + 1], [carry], [raw])
                        CP("act", raw[0:M, 1:T + 1], ps_ap, [ps_tile], [raw])
                        CP("pool", carry[0:M, ci:ci + 1], raw[0:M, T:T + 1], [raw], [carry])
                        TT("dve", dtmp[0:M, :], raw[0:M, 0:T], raw[0:M, 1:T + 1], ALU.subtract, [raw], [dtmp])
                        STT("dve", dest, dtmp[0:M, :], vecs[0:M, mucol:mucol + 1], raw[0:M, 1:T + 1], ALU.mult, ALU.add,
                            [dtmp, raw, vecs], [dest_tile])

                    s, w = wload("mix_lora")
                    pbk = nb()
                    for k in range(8):
                        MM(pbk[:], w[:, k, 0:128], xn_t[:, k, :], k == 0, k == 7, [s, xnT[k]], [pbk])
                    lerp(pbk[:], pbk, 128, 12, V_MU_XWA, xwa32[:], xwa32)
                    ACT(txa[0:64, :], xwa32[0:64, :], AF.Tanh, [xwa32], [txa])
                    CP("pool", txa[64:128, :], xwa32[64:128, :], [xwa32], [txa])
                    pbk = nb()
                    for k in range(8):
                        MM(pbk[:], w[:, k, 128:256], xn_t[:, k, :], k == 0, k == 7, [s, xnT[k]], [pbk])
                    lerp(pbk[:], pbk, 128, 13, V_MU_XG0, xwa32[:], xwa32)
                    ACT(sxg[:, 0, :], xwa32[:], AF.Sigmoid, [xwa32], [sxg])
                    pbk = nb()
                    for k in range(8):
                        MM(pbk[0:32, :], w[:, k, 256:288], xn_t[:, k, :], k == 0, k == 7, [s, xnT[k]], [pbk])
                    lerp(pbk[0:32, :], pbk, 32, 14, V_MU_XG1, xwa32[0:32, :], xwa32)
                    ACT(sxg[0:32, 1, :], xwa32[0:32, :], AF.Sigmoid, [xwa32], [sxg])

                    for fc in range(4):
                        P.mute = stages < 3 or rwsub < 2
                        s, w = wload("mix_rw%d" % fc)
                        for which, dest, dtile, ci, mucol in ((0, r32[:], r32, fc, V_MU_R + fc),
                                                              (1, k32[:], k32, 4 + fc, V_MU_K + fc),
                                                              (2, vTb[:], vTb, 8 + fc, V_MU_V + fc)):
                            pbk = nb()
                            for k in range(8):
                                MM(pbk[:], w[:, k, which * 128:(which + 1) * 128], xn_t[:, k, :], k == 0, k == 7,
                                   [s, xnT[k]], [pbk])
                            lerp(pbk[:], pbk, 128, ci, mucol, dest, dtile)
                        cs = slice(fc * 128, (fc + 1) * 128)
                        pbk = nb()
                        MM(pbk[:], smallw[0:64, 0, cs], txa[0:64, :], True, True, [smallw, txa], [pbk])
                        ACT(sig[:], pbk[:], AF.Sigmoid, [pbk, vecs], [sig], bias=vcol(V_W0 + fc))
                        P.op("dve", lambda e: e.tensor_tensor_scan(out=cum[:], data0=resetm[:], data1=sig[:], initial=0.0,
                                                                   op0=ALU.mult, op1=ALU.add), [resetm, sig], [cum])
                        TT("pool", cumx[:], cum[:], sig[:], ALU.subtract, [cum, sig], [sig])
                        ACT(eW[:], cum[:], AF.Exp, [cum], [eW], scale=-CDEC)
                        ACT(eWx[:], cumx[:], AF.Exp, [cumx], [eWx], scale=-CDEC)
                        ACT(eWi[:], cum[:], AF.Exp, [cum], [eWi], scale=CDEC)
                        CP("pool", WC[:, 0:4], eW[:].rearrange("p (c t) -> p c t", t=128)[:, :, 127], [eW], [WC])
                        pbk = nb()
                        MM(pbk[:], smallw[64:128, 0, cs], txa[64:128, :], True, True, [smallw, txa], [pbk])
                        ACT(a32[:], pbk[:], AF.Sigmoid, [pbk, vecs], [a32], bias=vcol(V_A0 + fc))
                        TS1("dve", kk32[:], k32[:], vcol(V_KK + fc), ALU.mult, [k32, vecs], [kk32])
                        q = sqb[sqrot[0] % 2]
                        sqrot[0] += 1
                        ACT(q[:], kk32[:], AF.Square, [kk32], [q])
                        pbk = nb()
                        MM(pbk[:], blk_bf[:], q[:], True, True, [blk_bf, q], [pbk])
                        ACT(nrm[:], pbk[:], AF.Sqrt, [pbk], [nrm])
                        TS1("dve", nrm[:], nrm[:], 1e-12, ALU.max, [nrm], [nrm])
                        RECIP(nrm[:], nrm[:], [nrm], [nrm])
                        TT("dve", kkn[:], kk32[:], nrm[:], ALU.mult, [kk32, nrm], [kkn])
                        TS("pool", kmod[:], a32[:], vcol(V_KA + fc), vcol(V_KA + fc), ALU.mult, ALU.subtract, [a32, vecs], [kmod])
                        STT("pool", kmod[:], kmod[:], 1.0, k32[:], ALU.add, ALU.mult, [kmod, k32], [kmod])
                        ar4 = ARt[:]
                        STT("dve", ar4[:, :, 0:128], kkn[:].rearrange("p (c t) -> p c t", t=128), -1.0,
                            eWx[:].rearrange("p (c t) -> p c t", t=128), ALU.mult, ALU.mult, [kkn, eWx], [ARt])
                        TT("pool", ar4[:, :, 128:256], r32[:].rearrange("p (c t) -> p c t", t=128),
                           eW[:].rearrange("p (c t) -> p c t", t=128), ALU.mult, [r32, eW], [ARt])
                        TT("dve", b32[:], kkn[:], a32[:], ALU.mult, [kkn, a32], [b32])
                        TT("dve", BTt[:], b32[:], eWi[:], ALU.mult, [b32, eWi], [BTt])
                        TT("pool", KTl[:], kmod[:], eWi[:], ALU.mult, [kmod, eWi], [KTl])
                        STT("dve", rkr_t[:, fc, :], r32[:], vcol(V_RK + fc), kmod[:], ALU.mult, ALU.mult,
                            [r32, kmod, vecs], [rkr[fc]])
                        P.mute = stages < 3 or rwsub < 3
                        for src, stile, dst_ap, dtile in ((BTt, BTt, Btok[:], Btok), (KTl, KTl, Ktok[:], Ktok),
                                                          (vTb, vTb, Vtok_t[:, :, cs], Vtok[fc])):
                            pbk = nb()
                            pbb = pbk[:].bitcast(BF16)
                            for tc in range(4):
                                TR(pbb[:, tc * 128:(tc + 1) * 128], src[:, tc * 128:(tc + 1) * 128], identb[:],
                                   [stile, identb], [pbk])
                            CP("act", dst_ap, pbb[:, 0:512].rearrange("p (c f) -> p c f", f=128), [pbk], [dtile])
                        P.mute = stages < 3 or rwsub < 4
                        for tc in range(4):
                            tcs = slice(tc * 128, (tc + 1) * 128)
                            pm = nb(); pk = nb(); pn = nb()
                            for hh in range(2):
                                po = hh * 64
                                MM(pm[:, hh * 256:(hh + 1) * 256], BTt[po:po + 64, tcs], ARt[po:po + 64, tc, :], True, True,
                                   [BTt, ARt], [pm])
                                MM(pk[:, hh * 256:(hh + 1) * 256], KTl[po:po + 64, tcs], ARt[po:po + 64, tc, :], True, True,
                                   [KTl, ARt], [pk])
                                MM(pn[:, hh * 128:(hh + 1) * 128], ARt[po:po + 64, tc, 0:128], BTt[po:po + 64, tcs], True, True,
                                   [ARt, BTt], [pn])
                            TT("dve", MMb[:].rearrange("p a b -> p (a b)"), pm[:], maskAB[:], ALU.mult, [pm, maskAB], [MMb])
                            TT("dve", KMb[:].rearrange("p a b -> p (a b)"), pk[:], maskAB[:], ALU.mult, [pk, maskAB], [KMb])
                            X = Xb[0]; XT = XTb[0]
                            TT("dve", X[:].rearrange("p a b -> p (a b)"), pn[:, 0:256], maskN2[:], ALU.mult, [pn, maskN2], [X])
                            CP("pool", XT[:], MMb[:, :, 0:128], [MMb], [XT])
                            pr = nb()
                            MM(pr[:, 0:128], ARt[:, tc, 0:128], S16[:, fc, :], True, False, [ARt, S16], [pr])
                            for hh in range(2):
                                MM(pr[:, hh * 64:(hh + 1) * 64], KMb[:, hh, 0:128],
                                   Vtok_t[:, tc, fc * 128 + hh * 64:fc * 128 + hh * 64 + 64], False, hh == 1,
                                   [KMb, Vtok[fc]], [pr])
                            CP("dve", SA32[:], pr[:, 0:128], [pr], [SA32])
                            CP("act", SA16[:], SA32[:], [SA32], [SA16])
                            for i in range(7):
                                X = Xb[i % 2]; XT = XTb[i % 2]
                                pd = nb()
                                for hh in range(2):
                                    MM(pd[:, hh * 64:(hh + 1) * 64], XT[:, hh, :], SA16[:, hh * 64:(hh + 1) * 64], True, True,
                                       [XT, SA16], [pd])
                                TT("dve", SA32[:], SA32[:], pd[:, 0:128], ALU.add, [SA32, pd], [SA32])
                                CP("act", SA16[:], SA32[:], [SA32], [SA16])
                                if i < 6:
                                    Xn = Xb[(i + 1) % 2]; XTn = XTb[(i + 1) % 2]
                                    px = nb(); pxt = nb()
                                    for hh in range(2):
                                        MM(px[:, hh * 128:(hh + 1) * 128], XT[:, hh, :], X[:, hh, :], True, True, [XT, X], [px])
                                        MM(pxt[:, hh * 128:(hh + 1) * 128], X[:, hh, :], XT[:, hh, :], True, True, [XT, X], [pxt])
                                    CP("pool" if False else "dve", Xn[:].rearrange("p a b -> p (a b)"), px[:, 0:256], [px], [Xn])
                                    CP("act", XTn[:].rearrange("p a b -> p (a b)"), pxt[:, 0:256], [pxt], [XTn])
                            py = nb()
                            MM(py[:, 0:128], ARt[:, tc, 128:256], S16[:, fc, :], True, False, [ARt, S16], [py])
                            for hh in range(2):
                                hs = slice(hh * 64, (hh + 1) * 64)
                                MM(py[:, hs], MMb[:, hh, 128:256], SA16[:, hs], False, False, [MMb, SA16], [py])
                                MM(py[:, hs], KMb[:, hh, 128:256], Vtok_t[:, tc, fc * 128 + hh * 64:fc * 128 + hh * 64 + 64],
                                   False, hh == 1, [KMb, Vtok[fc]], [py])
                            CP("act", Yall_t[:, tc, cs], py[:, 0:128], [py], [Yall[tc]])
                            pst = nb()
                            MM(pst[:, 0:128], Btok[:, tc, :], SA16[:], True, False, [Btok, SA16], [pst])
                            MM(pst[:, 0:128], Ktok[:, tc, :], Vtok_t[:, tc, cs], False, True, [Ktok, Vtok[fc]], [pst])
                            for hh in range(2):
                                ps_ = slice(hh * 64, (hh + 1) * 64)
                                TS1("pool", S32[ps_, fc, ps_], S32[ps_, fc, ps_], WC[ps_, tc:tc + 1], ALU.mult, [S32, WC], [S32])
                                STT("dve", S32[ps_, fc, ps_], pst[ps_, ps_], WC[ps_, tc:tc + 1], S32[ps_, fc, ps_],
                                    ALU.mult, ALU.add, [pst, WC, S32], [S32])
                            CP("act", S16[:, fc, :], S32[:, fc, :], [S32], [S16])
                    P.mute = stages < 3 or rwsub < 5
                    for tc in range(4):
                        tcs = slice(tc * 128, (tc + 1) * 128)
                        pg = nb()
                        MM(pg[:], sxg[:, 0, tcs], smallw[:, 1, :], True, False, [sxg, smallw], [pg])
                        MM(pg[:], sxg[0:32, 1, tcs], smallw[0:32, 2, :], False, True, [sxg, smallw], [pg])
                        pbn = nb()
                        for fc in range(4):
                            MM(pbn[:, fc * 2:fc * 2 + 2], rkr_t[:, fc, tcs], ind2[:], True, True, [rkr[fc], ind2], [pbn])
                        CP("act", st8[:, 5, :], pbn[:, 0:8], [pbn], [st8])
                        Y = Yall_t[:, tc, :]
                        Y3 = Y.rearrange("p (h d) -> p h d", d=64)
                        P.op("dve", lambda e, Y3=Y3: e.tensor_reduce(out=st8[:, 0, :], in_=Y3, axis=AX.X, op=ALU.add),
                             [Yall[tc]], [st8])
                        ACT(ysq[:], Y, AF.Square, [Yall[tc]], [ysq])
                        P.op("dve", lambda e: e.tensor_reduce(out=st8[:, 1, :], in_=ysq[:].rearrange("p (h d) -> p h d", d=64),
                                                              axis=AX.X, op=ALU.add), [ysq], [st8])
                        TS1("dve", st8[:, 2, :], st8[:, 0, :], 1.0 / 64, ALU.mult, [st8], [st8])
                        TT("dve", st8[:, 3, :], st8[:, 2, :], st8[:, 2, :], ALU.mult, [st8], [st8])
                        STT("dve", st8[:, 3, :], st8[:, 1, :], 1.0 / 64, st8[:, 3, :], ALU.mult, ALU.subtract, [st8], [st8])
                        ACT(st8[:, 4, :], st8[:, 3, :], AF.Sqrt, [st8, epsA], [st8], bias=epsA[:, 2:3])
                        RECIP(st8[:, 4, :], st8[:, 4, :], [st8], [st8])
                        for hd in range(8):
                            hs = slice(hd * 64, (hd + 1) * 64)
                            TS("dve" if hd % 2 == 0 else "pool", yn[:, hs], Yall_t[:, tc, hs], st8[:, 2, hd:hd + 1],
                               st8[:, 4, hd:hd + 1], ALU.subtract, ALU.mult, [Yall[tc], st8], [yn])
                        TT("dve", yn[:], yn[:], bvecs[:, 0:512], ALU.mult, [yn, bvecs], [yn])
                        TT("pool", yn[:], yn[:], bvecs[:, 512:1024], ALU.add, [yn, bvecs], [yn])
                        for hd in range(8):
                            hs = slice(hd * 64, (hd + 1) * 64)
                            STT("dve" if hd % 2 == 0 else "pool", yn[:, hs], Vtok_t[:, tc, hs], st8[:, 5, hd:hd + 1], yn[:, hs],
                                ALU.mult, ALU.add, [Vtok[hd // 2], st8, yn], [yn])
                        TT("dve", rwo[:], pg[:], yn[:], ALU.mult, [pg, yn], [rwo])
                        pbk = nb()
                        pbb = pbk[:].bitcast(BF16)
                        for fc in range(4):
                            TR(pbb[:, fc * 128:(fc + 1) * 128], rwo[:, fc * 128:(fc + 1) * 128], identb[:], [rwo, identb], [pbk])
                        CP("act", mixcat_t[:, 4:8, tcs], pbb[:, 0:512].rearrange("p (c t) -> p c t", t=128), [pbk], mixcat[4:8])
                    P.mute = False
                    if dd:
                        dump("sbo", mixcat_t[:], mixcat)
                    for g in range(2):
                        s, w = wload("mixo%d" % g)
                        for cc in range(4):
                            c = g * 4 + cc
                            pbk = nb()
                            for k in range(8):
                                MM(pbk[:], w[:, k, cc * 128:(cc + 1) * 128], mixcat_t[:, k, :], k == 0, k == 7,
                                   [s, mixcat[k]], [pbk])
                            CP("act", tmp8[:, c, :], pbk[:], [pbk], [tmpT[c]])
                    post_residual(V_MIXPOST, False)
                if dd:
                    dump("h2", h_t[:], hT)
                if stages >= 4:
                    BAR()
                    rmsnorm_to_xn(V_MEMPRE)
                    s, w = wload("memq")
                    for c in range(4):
                        pbk = nb()
                        for k in range(8):
                            MM(pbk[:], w[:, k, c * 128:(c + 1) * 128], xn_t[:, k, :], k == 0, k == 7, [s, xnT[k]], [pbk])
                        P.op("act", lambda e, c=c, pbk=pbk: e.mul(out=qTm_t[:, c, :], in_=pbk[:], mul=128.0 ** -0.5),
                             [pbk], [qTm[c]])
                    for hd in range(4):
                        es = []
                        for mb in range(2):
                            pbk = nb()
                            MM(pbk[:], kmT[:, hd, mb * 128:(mb + 1) * 128], qTm_t[:, hd, :], True, True, [kmT, qTm[hd]], [pbk])
                            Et = Em[(hd % 2) * 2 + mb]
                            ACT(Et[:], pbk[:], AF.Exp, [pbk], [Et])
                            es.append(Et)
                        pS = nb()
                        MM(pS[:], ones_bf[:], es[0][:], True, False, [ones_bf, es[0]], [pS])
                        MM(pS[:], ones_bf[:], es[1][:], False, True, [ones_bf, es[1]], [pS])
                        pO = nb()
                        MM(pO[:], vmT[:, 0, hd * 128:(hd + 1) * 128], es[0][:], True, False, [vmT, es[0]], [pO])
                        MM(pO[:], vmT[:, 1, hd * 128:(hd + 1) * 128], es[1][:], False, True, [vmT, es[1]], [pO])
                        RECIP(rsb[:], pS[:], [pS], [rsb])
                        TT("dve", oTm_t[:, hd, :], pO[:], rsb[:], ALU.mult, [pO, rsb], [oTm[hd]])
                    s, w = wload("memo")
                    for c in range(8):
                        pbk = nb()
                        for k in range(4):
                            MM(pbk[:], w[:, k, c * 128:(c + 1) * 128], oTm_t[:, k, :], k == 0, k == 3, [s, oTm[k]], [pbk])
                        CP("act", tmp8[:, c, :], pbk[:], [pbk], [tmpT[c]])
                    post_residual(V_MEMPOST, False)
                if dd:
                    dump("h3", h_t[:], hT)
                if stages >= 5:
                    ffn("f2", V_F2PRE, V_F2POST)
                for tb in range(4):
                    for g in range(2):
                        pbk = nb()
                        for cc in range(4):
                            c = g * 4 + cc
                            TR(pbk[:, cc * 128:(cc + 1) * 128], h_t[:, c, tb * 128:(tb + 1) * 128], ident[:],
                               [hT[c], ident], [pbk])
                        CP("act" if g == 0 else "dve", ost[:, tb, g * 512:(g + 1) * 512], pbk[:], [pbk],
                           [tmpT[2 * tb], tmpT[2 * tb + 1]])
                P.dma("sp", [lambda e, b=b, t0=t0: e.dma_start(
                    out=out_d[b, t0:t0 + T, :].rearrange("(n p) f -> p n f", p=128), in_=ost)], "ost",
                    reads=tmpT, final=True)
        P.emit()
    return nc


_CACHE = {}


def kernel(**inputs):
    inp = {k: np.asarray(v) for k, v in inputs.items()}
    wf = _host_weights(inp)
    vecs = _host_vecs(inp)
    small = _host_small(inp)
    bv = _host_bvecs(inp)
    if "nc" not in _CACHE:
        _CACHE["nc"] = build_nc()
    nc = _CACHE["nc"]
    in_maps = []
    for c in range(8):
        in_maps.append({
            "x": np.ascontiguousarray(inp["x"][c * NB:(c + 1) * NB]),
            "mem": np.ascontiguousarray(inp["mem"][c * NB:(c + 1) * NB]),
            "wf": wf, "vecs": vecs, "small": small, "bvecs": bv,
        })
    res = run_bass_kernel_spmd(nc, in_maps, core_ids=list(range(8)))
    out = np.concatenate([np.asarray(r["out"]) for r in res.results], axis=0)
    return out.astype(np.float32)
```

```python
import contextlib
import math
import numpy as np
import concourse.bass as bass
import concourse.mybir as mybir
from concourse.bass_utils import run_bass_kernel_spmd

F32 = mybir.dt.float32
BF16 = mybir.dt.bfloat16
AF = mybir.ActivationFunctionType
ALU = mybir.AluOpType
AX = mybir.AxisListType

SAME_ENGINE_SYNC = ('act', 'dve', 'pool')
ENGS = ("pe", "act", "dve", "pool", "sp")

NB = 2
SEQ = 2048
D = 1024
T = 512
NT = SEQ // T
DFF = 2816
NJ = DFF // 128
MEM = 256
CDEC = math.exp(-0.5)
NSLOT = 3
SLOT_ELEMS = 4096


class Tile:
    __slots__ = ("t", "name", "last_w", "readers", "excl")

    def __init__(self, t, name="", excl=False):
        self.t = t
        self.name = name
        self.last_w = None
        self.readers = []
        self.excl = excl

    def __getitem__(self, k):
        return self.t[k]


class _Op:
    __slots__ = ("eng", "fn", "deps", "waits", "signal", "sigval", "dma", "idx", "force")


class Prog:
    def __init__(self, nc):
        self.nc = nc
        self.ops = {e: [] for e in ENGS}
        self.dma_sems = {}
        self.dma_tok = []
        self.final_tokens = []

    def _deps_for(self, reads, writes):
        deps = []
        for t in reads:
            if t.last_w is not None:
                deps.append(t.last_w)
            if t.excl:
                deps.extend(t.readers)
        for t in writes:
            if t.last_w is not None:
                deps.append(t.last_w)
            deps.extend(t.readers)
        return deps

    def _commit(self, tok, reads, writes):
        for t in reads:
            t.readers.append(tok)
        for t in writes:
            t.last_w = tok
            t.readers = []

    def _new(self, eng, fn, reads, writes):
        o = _Op()
        o.eng = eng
        o.fn = fn
        o.deps = self._deps_for(reads, writes)
        o.waits = []
        o.signal = False
        o.sigval = 0
        o.dma = None
        o.force = -1
        o.idx = len(self.ops[eng])
        self.ops[eng].append(o)
        return o

    mute = False

    def op(self, eng, fn, reads=(), writes=(), force_prev=False):
        if self.mute:
            return None
        o = self._new(eng, fn, reads, writes)
        if force_prev and o.idx > 0:
            for j in range(o.idx - 1, -1, -1):
                pj = self.ops[eng][j]
                if pj.fn is not None and pj.dma is None:
                    o.force = j
                    o.deps.append(("e", eng, j))
                    break
        tok = ("e", eng, o.idx)
        self._commit(tok, reads, writes)
        return tok

    def dma(self, eng, fns, semkey, reads=(), writes=(), final=False):
        if self.mute:
            return None
        o = self._new(eng, fns, reads, writes)
        ent = self.dma_sems.setdefault(semkey, [len(self.dma_sems), 0])
        ent[1] += 16 * len(fns)
        did = len(self.dma_tok)
        self.dma_tok.append((semkey, ent[1]))
        o.dma = semkey
        tok = ("d", did)
        self._commit(tok, reads, writes)
        if final:
            self.final_tokens.append(tok)
        return tok

    def barrier(self, extra=()):
        engs = ("pe", "act", "dve", "pool")
        deps = list(extra)
        for Pn in engs:
            for o in reversed(self.ops[Pn]):
                if o.fn is not None and o.dma is None:
                    deps.append(("e", Pn, o.idx))
                    break
        for E in engs:
            o = self._new(E, None, (), ())
            o.deps = list(deps)

    def emit(self, final_eng="sp"):
        nc = self.nc
        fo = self._new(final_eng, None, (), ())
        fo.deps = list(self.final_tokens)
        for E in ENGS:
            waited = {}
            for o in self.ops[E]:
                need = {}
                for tok in o.deps:
                    if tok[0] == "e":
                        _, Pn, idx = tok
                        if Pn == E and ((E not in SAME_ENGINE_SYNC and idx != o.force) or idx >= o.idx):
                            continue
                        key = ("e", Pn)
                        if need.get(key, (-1, None))[0] < idx:
                            need[key] = (idx, tok)
                    else:
                        semkey, val = self.dma_tok[tok[1]]
                        key = ("d", semkey)
                        if need.get(key, (-1, None))[0] < val:
                            need[key] = (val, tok)
                for key, (v, tok) in need.items():
                    if waited.get(key, -1) >= v:
                        continue
                    waited[key] = v
                    if tok[0] == "e":
                        self.ops[tok[1]][tok[2]].signal = True
                    o.waits.append(tok)
        for E in ENGS:
            c = 0
            for o in self.ops[E]:
                if o.signal:
                    c += 1
                o.sigval = c
        with contextlib.ExitStack() as st:
            esem = {E: st.enter_context(nc.semaphore("s_" + E)) for E in ENGS}
            dsem = {}
            for k, (i, _) in self.dma_sems.items():
                dsem[k] = st.enter_context(nc.semaphore("d%d" % i))
            block = st.enter_context(nc.Block())

            def run(E, eng):
                for o in self.ops[E]:
                    for tok in o.waits:
                        if tok[0] == "e":
                            _, Pn, idx = tok
                            eng.wait_ge(esem[Pn], self.ops[Pn][idx].sigval)
                        else:
                            semkey, val = self.dma_tok[tok[1]]
                            eng.wait_ge(dsem[semkey], val)
                    if o.fn is None:
                        continue
                    if o.dma is not None:
                        for f in o.fn:
                            f(eng).then_inc(dsem[o.dma], 16)
                    else:
                        ins = o.fn(eng)
                        if o.signal:
                            ins.then_inc(esem[E], 1)

            @block.tensor
            def _(eng):
                run("pe", eng)

            @block.scalar
            def _(eng):
                run("act", eng)

            @block.vector
            def _(eng):
                run("dve", eng)

            @block.gpsimd
            def _(eng):
                run("pool", eng)

            @block.sync
            def _(eng):
                run("sp", eng)


RW0 = 1536


def _piece(W, cols):
    K = W.shape[0]
    kc = K // 128
    sub = W[:, cols]
    return np.ascontiguousarray(sub.reshape(kc, 128, len(cols)).transpose(1, 0, 2))


def _plan():
    pl = []
    for f in ("f1", "f2"):
        for i in range(NJ // 2):
            pl.append((f + "_in%d" % i, 8, 512))
        for c in range(8):
            pl.append((f + "_out%d" % c, NJ, 128))
    pl += [("mix_q", 8, 512), ("mix_k", 8, 512), ("mix_v", 8, 512), ("mix_lora", 8, 288)]
    for fc in range(4):
        pl.append(("mix_rw%d" % fc, 8, 384))
    pl += [("mixo0", 8, 512), ("mixo1", 8, 512), ("memq", 8, 512), ("memo", 4, 1024),
           ("memk", 8, 512), ("memv", 8, 512)]
    return pl


def _offsets():
    off = {}
    o = 0
    for name, kc, gc in _plan():
        off[name] = (o, kc, gc)
        o += kc * gc
    return off, o


def _host_weights(inp):
    ar = np.arange
    pieces = {}
    for f, wi, wo in (("f1", "ffn1_w_in", "ffn1_w_out"), ("f2", "ffn2_w_in", "ffn2_w_out")):
        Wi = inp[wi][0]
        Wo = inp[wo][0]
        for i in range(NJ // 2):
            cols = np.concatenate([ar(128) + (2 * i) * 128, ar(128) + DFF + (2 * i) * 128,
                                   ar(128) + (2 * i + 1) * 128, ar(128) + DFF + (2 * i + 1) * 128])
            pieces[f + "_in%d" % i] = _piece(Wi, cols)
        for c in range(8):
            pieces[f + "_out%d" % c] = _piece(Wo, ar(128) + c * 128)
    Wm = inp["mix_w_in"][0]
    pieces["mix_q"] = _piece(Wm, ar(512))
    pieces["mix_k"] = _piece(Wm, ar(512) + 512)
    pieces["mix_v"] = _piece(Wm, ar(512) + 1024)
    pieces["mix_lora"] = _piece(Wm, ar(288) + RW0 + 1536)
    for fc in range(4):
        cols = np.concatenate([ar(128) + RW0 + fc * 128, ar(128) + RW0 + 512 + fc * 128,
                               ar(128) + RW0 + 1024 + fc * 128])
        pieces["mix_rw%d" % fc] = _piece(Wm, cols)
    Wo = inp["mix_w_out"][0]
    pieces["mixo0"] = _piece(Wo, ar(512))
    pieces["mixo1"] = _piece(Wo, ar(512) + 512)
    pieces["memq"] = _piece(inp["mem_w_q"][0], ar(512))
    pieces["memo"] = _piece(inp["mem_w_o"][0], ar(1024))
    pieces["memk"] = _piece(inp["mem_w_kv"][0], ar(512))
    pieces["memv"] = _piece(inp["mem_w_kv"][0], ar(512) + 512)
    off, total = _offsets()
    wf = np.empty((128, total), np.float32)
    for name, (o, kc, gc) in off.items():
        wf[:, o:o + kc * gc] = pieces[name].reshape(128, kc * gc)
    return wf


V_F1PRE, V_F1POST, V_MIXPRE, V_MIXPOST, V_MEMPRE, V_MEMPOST, V_F2PRE, V_F2POST = [8 * i for i in range(8)]
V_MU_R, V_MU_K, V_MU_V = 64, 68, 72
V_MU_XWA, V_MU_XG0, V_MU_XG1 = 76, 77, 78
V_W0, V_A0, V_KK, V_KA, V_RK, V_SBG = 79, 83, 87, 91, 95, 99
V_MEMKV = 103
NV = 112


def _host_vecs(inp):
    v = np.zeros((128, NV), np.float32)

    def put(col, vec):
        n = vec.shape[0]
        nch = (n + 127) // 128
        for c in range(nch):
            seg = vec[c * 128:(c + 1) * 128]
            v[:seg.shape[0], col + c] = seg

    put(V_F1PRE, inp["ffn1_pre"][0]); put(V_F1POST, inp["ffn1_post"][0])
    put(V_MIXPRE, inp["mix_pre"][0]); put(V_MIXPOST, inp["mix_post"][0])
    put(V_MEMPRE, inp["mem_pre"][0]); put(V_MEMPOST, inp["mem_post"][0])
    put(V_F2PRE, inp["ffn2_pre"][0]); put(V_F2POST, inp["ffn2_post"][0])
    mu = inp["rwkv_mu"][0]
    put(V_MU_R, mu[0:512]); put(V_MU_K, mu[512:1024]); put(V_MU_V, mu[1024:1536])
    put(V_MU_XWA, mu[1536:1664]); put(V_MU_XG0, mu[1664:1792]); put(V_MU_XG1, mu[1792:1824])
    put(V_W0, inp["rwkv_w0"][0]); put(V_A0, inp["rwkv_a0"][0])
    put(V_KK, inp["rwkv_k_k"][0]); put(V_KA, inp["rwkv_k_a"][0])
    put(V_RK, inp["rwkv_r_k"][0].reshape(-1)); put(V_SBG, inp["sb_out_g"][0])
    put(V_MEMKV, inp["mem_kv_g"][0])
    return v


def _host_small(inp):
    s = np.zeros((128, 3, 512), np.float32)
    s[0:64, 0] = inp["rwkv_w2"][0]
    s[64:128, 0] = inp["rwkv_a2"][0]
    g2 = inp["rwkv_g2"][0]
    s[:, 1] = g2[0:128]
    s[0:32, 2] = g2[128:160]
    return s


def _host_bvecs(inp):
    b = np.empty((128, 1024), np.float32)
    b[:, 0:512] = inp["rwkv_lnx_g"][0][None, :]
    b[:, 512:1024] = inp["rwkv_lnx_b"][0][None, :]
    return b


def build_nc(ntiles_limit=None, dbg=False, stages=99, rwsub=99):
    nc = bass.Bass("TRN2", target_bir_lowering=False)
    off, WTOT = _offsets()
    x_d = nc.dram_tensor("x", [NB, SEQ, D], F32, kind="ExternalInput").ap()
    mem_d = nc.dram_tensor("mem", [NB, MEM, D], F32, kind="ExternalInput").ap()
    wf_d = nc.dram_tensor("wf", [128, WTOT], F32, kind="ExternalInput").ap()
    vecs_d = nc.dram_tensor("vecs", [128, NV], F32, kind="ExternalInput").ap()
    small_d = nc.dram_tensor("small", [128, 3, 512], F32, kind="ExternalInput").ap()
    bvecs_d = nc.dram_tensor("bvecs", [128, 1024], F32, kind="ExternalInput").ap()
    out_d = nc.dram_tensor("out", [NB, SEQ, D], F32, kind="ExternalOutput").ap()
    wb_d = nc.dram_tensor("wbf", [128, WTOT], BF16, kind="Internal").ap()
    dbg_d = {}
    if dbg:
        for nm in ("h1", "q", "sbo", "rwo", "h2", "h3"):
            dbg_d[nm] = nc.dram_tensor("dbg_" + nm, [128, 8, T], F32, kind="ExternalOutput").ap()

    P = Prog(nc)
    with contextlib.ExitStack() as st:
        def sb(name, shape, dt=F32):
            return st.enter_context(nc.sbuf_tensor("sb_" + name, shape, dt))

        def sbt(name, shape, dt=F32):
            return Tile(sb(name, shape, dt), name)

        h_t = sb("h", [128, 8, T])
        hT = [Tile(h_t, "h%d" % c) for c in range(8)]
        xn_t = sb("xn", [128, 8, T], BF16)
        xnT = [Tile(xn_t, "xn%d" % c) for c in range(8)]
        tmp_t = sb("tmp8", [128, 8 * T])
        tmpT = [Tile(tmp_t, "tmp%d" % c) for c in range(8)]
        tmp8 = tmp_t[:].rearrange("p (c t) -> p c t", t=T)
        ost = tmp_t[:].rearrange("p (n f) -> p n f", f=D)
        stg = sbt("stg", [128, 2, D])
        wslot = [sbt("wslot%d" % i, [128, SLOT_ELEMS], BF16) for i in range(NSLOT)]
        KTt = sb("KT", [128, 4, SEQ], BF16)
        KT = [Tile(KTt, "KT%d" % c) for c in range(4)]
        Vt = sb("V", [128, 16, 512], BF16)
        VT = [Tile(Vt, "V%d" % c) for c in range(16)]
        vecs = sbt("vecs", [128, NV])
        bvecs = sbt("bvecs", [128, 1024])
        smallw = sbt("smallw", [128, 3, 512], BF16)
        kmT = sbt("kmT", [128, 4, MEM], BF16)
        vmT = sbt("vm", [128, 2, 512], BF16)
        S32 = sbt("S32", [128, 4, 128])
        S16 = sbt("S16", [128, 4, 128], BF16)
        carry = sbt("carry", [128, 16])
        rstd = sbt("rstd", [128, T])
        sqb = [sbt("sqb%d" % i, [128, T], BF16) for i in range(2)]
        ident = sbt("ident", [128, 128])
        identb = sbt("identb", [128, 128], BF16)
        onesf = sbt("onesf", [128, 512])
        ones_bf = sbt("ones_bf", [128, 128], BF16)
        blk_bf = sbt("blk_bf", [128, 128], BF16)
        triU_f = sbt("triU_f", [128, 128])
        triU_bf = sbt("triU_bf", [128, 128], BF16)
        mask4 = sbt("mask4", [128, 4, 512], BF16)
        maskAB = sbt("maskAB", [128, 512])
        maskN2 = sbt("maskN2", [128, 256])
        resetm = sbt("resetm", [128, 512])
        ind2f = sbt("ind2f", [128, 2])
        ind2 = sbt("ind2", [128, 2], BF16)
        epsA = sbt("epsA", [128, 4])

        ARENA_BYTES = 72 * 1024
        arena = sb("arena", [128, ARENA_BYTES // 4])
        apos = [0]

        def aset(off):
            apos[0] = off

        def ar(shape, dt=F32):
            n = 1
            for d_ in shape[1:]:
                n *= d_
            nbytes = n * (2 if dt == BF16 else 4)
            nbytes = (nbytes + 31) // 32 * 32
            a0 = apos[0]
            apos[0] += nbytes
            assert apos[0] <= ARENA_BYTES, ("arena overflow", apos[0])
            v = arena[:, a0 // 4:(a0 + nbytes) // 4]
            if dt == BF16:
                v = v.bitcast(BF16)
            v = v[:, 0:n]
            if len(shape) == 3:
                v = v.rearrange("p (a b) -> p a b", b=shape[2])
            if shape[0] < 128:
                v = v[0:shape[0]]
            return v

        def art(name, shape, dt=F32):
            return Tile(ar(shape, dt), name)

        aset(0)
        mixcat_t = ar([128, 8, T], BF16)
        mixcat = [Tile(mixcat_t, "mixcat%d" % c) for c in range(8)]
        MIX0 = apos[0]
        aset(0)
        hid_t = ar([128, NJ, T], BF16)
        hidT = [Tile(hid_t, "hid%d" % j) for j in range(NJ)]
        sgt = [art("sg%d" % i, [128, T]) for i in range(2)]
        aset(MIX0)
        QTt = ar([128, 4, T], BF16)
        QT = [Tile(QTt, "QT%d" % c) for c in range(4)]
        Ebuf = [art("E%d" % i, [128, T]) for i in range(2)]
        SPb = [art("SPb%d" % i, [128, T], BF16) for i in range(2)]
        t1b = [art("t1_%d" % i, [128, T]) for i in range(2)]
        Ab = [art("A%d" % i, [128, T], BF16) for i in range(2)]
        accb = [art("acc%d" % i, [128, T]) for i in range(2)]
        Oraw_t = ar([128, 4, T])
        Oraw = [Tile(Oraw_t, "Oraw%d" % c) for c in range(4)]
        aset(MIX0)
        rawb = [art("raw%d" % i, [128, T + 8]) for i in range(2)]
        dtmp = art("dtmp", [128, T])
        txa = art("txa", [128, T], BF16)
        sxg = art("sxg", [128, 2, T], BF16)
        r32 = art("r32", [128, T]); k32 = art("k32", [128, T])
        vTb = art("vTb", [128, T], BF16)
        sig = art("sig", [128, T]); cum = art("cum", [128, T])
        eW = art("eW", [128, T]); eWx = art("eWx", [128, T]); eWi = art("eWi", [128, T])
        WC = art("WC", [128, 8])
        a32 = art("a32", [128, T]); kk32 = art("kk32", [128, T]); kkn = art("kkn", [128, T])
        kmod = art("kmod", [128, T])
        ARt = art("AR", [128, 4, 256], BF16)
        BTt = art("BT", [128, T], BF16); KTl = art("KTl", [128, T], BF16)
        rkr_t = ar([128, 4, T], BF16)
        rkr = [Tile(rkr_t, "rkr%d" % c) for c in range(4)]
        Btok = art("Btok", [128, 4, 128], BF16); Ktok = art("Ktok", [128, 4, 128], BF16)
        Vtok_t = ar([128, 4, 512], BF16)
        Vtok = [Tile(Vtok_t, "Vtok%d" % c) for c in range(4)]
        MMb = art("MMb", [128, 2, 256], BF16); KMb = art("KMb", [128, 2, 256], BF16)
        Xb = [art("X%d" % i, [128, 2, 128], BF16) for i in range(2)]
        XTb = [art("XT%d" % i, [128, 2, 128], BF16) for i in range(2)]
        SA32 = art("SA32", [128, 128]); SA16 = art("SA16", [128, 128], BF16)
        Yall_t = ar([128, 4, 512])
        Yall = [Tile(Yall_t, "Yall%d" % c) for c in range(4)]
        st8 = art("st8", [128, 6, 8])
        rwo = art("rwo", [128, 512], BF16)
        cumx = sig
        xwa32 = r32
        ysq = sig
        yn = cum
        print("arena rwkv end", apos[0])
        nrm = dtmp
        b32 = kk32
        aset(0)
        qTm_t = ar([128, 4, T], BF16)
        qTm = [Tile(qTm_t, "qTm%d" % c) for c in range(4)]
        Em = [art("Em%d" % i, [128, T], BF16) for i in range(4)]
        oTm_t = ar([128, 4, T], BF16)
        oTm = [Tile(oTm_t, "oTm%d" % c) for c in range(4)]
        rsb = art("rsb", [128, T])
        memn_t = ar([128, 8, MEM], BF16)
        memn = [Tile(memn_t, "memn%d" % c) for c in range(8)]

        banks = [Tile(st.enter_context(nc.psum_tensor("pb%d" % i, [128, 512], F32)), "pb%d" % i, excl=True)
                 for i in range(8)]
        rot = [0]

        def nb():
            b = banks[rot[0] % 6]
            rot[0] += 1
            return b

        pe_mode = [None]

        def _cls(n):
            return 32 if n <= 32 else (64 if n <= 64 else 128)

        def MM(out, lhsT, rhs, start, stop, reads, writes):
            mode = ("mm", _cls(lhsT.shape[0]), _cls(lhsT.shape[-1]), int(lhsT.base_partition()))
            sw = pe_mode[0] is not None and pe_mode[0] != mode
            if not P.mute:
                pe_mode[0] = mode
            P.op("pe", lambda e: e.matmul(out, lhsT=lhsT, rhs=rhs, start=start, stop=stop), reads, writes,
                 force_prev=sw)

        def TR(out, in_, idn, reads, writes):
            mode = ("tr", str(in_.dtype))
            sw = pe_mode[0] is not None and pe_mode[0] != mode
            if not P.mute:
                pe_mode[0] = mode
            P.op("pe", lambda e: e.transpose(out, in_, idn), reads, writes, force_prev=sw)

        def ACT(out, in_, func, reads, writes, bias=None, scale=1.0):
            if bias is None:
                P.op("act", lambda e: e.activation(out=out, in_=in_, func=func, scale=scale), reads, writes)
            else:
                P.op("act", lambda e: e.activation(out=out, in_=in_, func=func, bias=bias, scale=scale), reads, writes)

        def CP(eng, out, in_, reads, writes):
            if eng == "act":
                P.op("act", lambda e: e.copy(out=out, in_=in_), reads, writes)
            else:
                P.op(eng, lambda e: e.tensor_copy(out=out, in_=in_), reads, writes)

        def TT(eng, out, in0, in1, op, reads, writes):
            P.op(eng, lambda e: e.tensor_tensor(out=out, in0=in0, in1=in1, op=op), reads, writes)

        def STT(eng, out, in0, scalar, in1, op0, op1, reads, writes):
            P.op("dve", lambda e: e.scalar_tensor_tensor(out=out, in0=in0, scalar=scalar, in1=in1, op0=op0, op1=op1),
                 reads, writes)

        def TS(eng, out, in0, s1, s2, op0, op1, reads, writes):
            P.op(eng, lambda e: e.tensor_scalar(out=out, in0=in0, scalar1=s1, scalar2=s2, op0=op0, op1=op1),
                 reads, writes)

        def TS1(eng, out, in0, s1, op0, reads, writes):
            P.op(eng, lambda e: e.tensor_single_scalar(out=out, in_=in0, scalar=s1, op=op0), reads, writes)

        def MSET(eng, out, val, writes):
            P.op(eng, lambda e: e.memset(out, val), (), writes)

        def RECIP(out, in_, reads, writes):
            P.op("dve", lambda e: e.reciprocal(out=out, in_=in_), reads, writes)

        def ASEL(out, in_, pattern, cmp, base, cm, reads, writes):
            P.op("pool", lambda e: e.affine_select(out=out, in_=in_, pattern=pattern, compare_op=cmp, fill=0.0,
                                                   base=base, channel_multiplier=cm), reads, writes)

        def vcol(c):
            return vecs[:, c:c + 1]

        MSET("pool", onesf[:], 1.0, [onesf])
        ASEL(ident[:], onesf[:, 0:128], [[-1, 128]], ALU.is_equal, 0, 1, [onesf], [ident])
        CP("dve", identb[:], ident[:], [ident], [identb])
        CP("dve", ones_bf[:], onesf[:, 0:128], [onesf], [ones_bf])
        MSET("pool", blk_bf[:], 0.0, [blk_bf])
        MSET("pool", blk_bf[0:64, 0:64], 1.0, [blk_bf])
        MSET("pool", blk_bf[64:128, 64:128], 1.0, [blk_bf])
        ASEL(triU_f[:], onesf[:, 0:128], [[-1, 128]], ALU.is_gt, 0, 1, [onesf], [triU_f])
        CP("dve", triU_bf[:], triU_f[:], [triU_f], [triU_bf])
        for r in range(4):
            ASEL(rstd[:], onesf[:], [[1, 512]], ALU.is_gt, -128 * r, -1, [onesf], [rstd])
            CP("dve", mask4[:, r, :], rstd[:], [rstd], [mask4])
        for hh in range(2):
            ASEL(maskAB[:, hh * 256:hh * 256 + 128], onesf[:, 0:128], [[1, 128]], ALU.is_gt, 0, -1, [onesf], [maskAB])
            ASEL(maskAB[:, hh * 256 + 128:hh * 256 + 256], onesf[:, 0:128], [[1, 128]], ALU.is_ge, 0, -1, [onesf], [maskAB])
            CP("pool", maskN2[:, hh * 128:(hh + 1) * 128], triU_f[:], [triU_f], [maskN2])
        MSET("pool", resetm[:], 1.0, [resetm])
        for c in range(4):
            MSET("pool", resetm[:, c * 128:c * 128 + 1], 0.0, [resetm])
        MSET("pool", ind2f[:], 0.0, [ind2f])
        MSET("pool", ind2f[0:64, 0:1], 1.0, [ind2f])
        MSET("pool", ind2f[64:128, 1:2], 1.0, [ind2f])
        CP("dve", ind2[:], ind2f[:], [ind2f], [ind2])
        MSET("pool", epsA[:, 0:1], 1e-6, [epsA])
        MSET("pool", epsA[:, 1:2], 4e-6, [epsA])
        MSET("pool", epsA[:, 2:3], 64e-5, [epsA])
        MSET("pool", epsA[:, 3:4], 1.0, [epsA])
        P.dma("sp", [lambda e: e.dma_start(out=vecs[:], in_=vecs_d)], "vecs", writes=[vecs])
        P.dma("sp", [lambda e: e.dma_start(out=bvecs[:], in_=bvecs_d)], "bvecs", writes=[bvecs])
        P.dma("pool", [lambda e: e.dma_start(out=smallw[:], in_=small_d)], "smallw", writes=[smallw])

        wpiece = {}
        for name, kc, gc in _plan():
            o, _, _ = off[name]
            n = kc * gc
            tl = Tile(None, "w_" + name)
            wpiece[name] = tl
            P.dma("pool", [lambda e, o=o, n=n: e.dma_start(out=wb_d[:, o:o + n], in_=wf_d[:, o:o + n])],
                  ("wc", name), writes=[tl])

        wrot = [0]

        def wload(name):
            o, kc, gc = off[name]
            n = kc * gc
            s = wslot[wrot[0] % NSLOT]
            key = ("ws", wrot[0] % NSLOT)
            wrot[0] += 1
            P.dma("sp", [lambda e: e.dma_start(out=s[:, 0:n], in_=wb_d[:, o:o + n])], key,
                  reads=[wpiece[name]], writes=[s])
            return s, s[:, 0:n].rearrange("p (k c) -> p k c", c=gc)

        sqrot = [0]

        def sumsq_bcast(srcs, src_tiles, n, ones_l, nfree=T):
            pbk = nb()
            for c in range(n):
                q = sqb[sqrot[0] % 2]
                sqrot[0] += 1
                ACT(q[:, 0:nfree], srcs[c], AF.Square, [src_tiles[c]], [q])
                MM(pbk[:, 0:nfree], ones_l[:], q[:, 0:nfree], c == 0, c == n - 1, [q, ones_l], [pbk])
            return pbk

        def rmsnorm_to_xn(gcol):
            pbk = sumsq_bcast([h_t[:, c, :] for c in range(8)], hT, 8, ones_bf)
            ACT(rstd[:], pbk[:], AF.Sqrt, [pbk, epsA], [rstd], bias=epsA[:, 0:1], scale=1.0 / D)
            RECIP(rstd[:], rstd[:], [rstd], [rstd])
            for c in range(8):
                STT("dve", xn_t[:, c, :], h_t[:, c, :], vcol(gcol + c), rstd[:], ALU.mult, ALU.mult,
                    [hT[c], rstd, vecs], [xnT[c]])

        def post_residual(gcol, half):
            pbk = sumsq_bcast([tmp8[:, c, :] for c in range(8)], tmpT, 8, ones_bf)
            if half:
                ACT(rstd[:], pbk[:], AF.Sqrt, [pbk, epsA], [rstd], bias=epsA[:, 1:2], scale=4.0 / D)
            else:
                ACT(rstd[:], pbk[:], AF.Sqrt, [pbk, epsA], [rstd], bias=epsA[:, 0:1], scale=1.0 / D)
            RECIP(rstd[:], rstd[:], [rstd], [rstd])
            for c in range(8):
                STT("dve", tmp8[:, c, :], tmp8[:, c, :], vcol(gcol + c), rstd[:], ALU.mult, ALU.mult,
                    [tmpT[c], rstd, vecs], [tmpT[c]])
                TT("pool", h_t[:, c, :], h_t[:, c, :], tmp8[:, c, :], ALU.add, [hT[c], tmpT[c]], [hT[c]])

        def ffn(f, pre, post):
            BAR()
            rmsnorm_to_xn(pre)
            for i in range(NJ // 2):
                s, w = wload(f + "_in%d" % i)
                for jj in range(2):
                    j = 2 * i + jj
                    pg = nb()
                    for k in range(8):
                        MM(pg[:], w[:, k, jj * 256:jj * 256 + 128], xn_t[:, k, :], k == 0, k == 7, [s, xnT[k]], [pg])
                    pu = nb()
                    for k in range(8):
                        MM(pu[:], w[:, k, jj * 256 + 128:jj * 256 + 256], xn_t[:, k, :], k == 0, k == 7,
                           [s, xnT[k]], [pu])
                    sg = sgt[j % 2]
                    ACT(sg[:], pg[:], AF.Silu, [pg], [sg])
                    TT("dve", hid_t[:, j, :], pu[:], sg[:], ALU.mult, [pu, sg], [hidT[j]])
            for c in range(8):
                s, w = wload(f + "_out%d" % c)
                po = nb()
                for k in range(NJ):
                    MM(po[:], w[:, k, :], hid_t[:, k, :], k == 0, k == NJ - 1, [s, hidT[k]], [po])
                CP("act", tmp8[:, c, :], po[:], [po], [tmpT[c]])
            post_residual(post, True)

        arena_dma = []

        def BAR():
            P.barrier(extra=arena_dma)

        def dump(nm, ap3, tiles):
            if dbg and nm in dbg_d:
                tok = P.dma("pool", [lambda e: e.dma_start(out=dbg_d[nm][:, 0:ap3.shape[1], :], in_=ap3)], ("dbg", nm),
                            reads=tiles, final=True)
                arena_dma.append(tok)

        tile_count = 0
        for b in range(NB):
            if stages >= 4:
                BAR()
                P.dma("sp", [lambda e, b=b: e.dma_start(out=stg[:], in_=mem_d[b].rearrange("(n p) f -> p n f", p=128))],
                      "stg", writes=[stg])
                memT32 = tmp_t[:, 0:8 * MEM].rearrange("p (c t) -> p c t", t=MEM)
                for n in range(2):
                    for g in range(2):
                        pbk = nb()
                        for cc in range(4):
                            c = g * 4 + cc
                            TR(pbk[:, cc * 128:(cc + 1) * 128], stg[:, n, c * 128:(c + 1) * 128], ident[:],
                               [stg, ident], [pbk])
                        CP("act", memT32[:, g * 4:g * 4 + 4, n * 128:(n + 1) * 128],
                           pbk[:].rearrange("p (a t) -> p a t", a=4), [pbk], [tmpT[0], tmpT[1], tmpT[2], tmpT[3]])
                mt = [tmpT[0], tmpT[1], tmpT[2], tmpT[3]]
                pbk = nb()
                for c in range(8):
                    q = sqb[sqrot[0] % 2]
                    sqrot[0] += 1
                    ACT(q[:, 0:MEM], memT32[:, c, :], AF.Square, mt, [q])
                    MM(pbk[:, 0:MEM], ones_bf[:], q[:, 0:MEM], c == 0, c == 7, [q, ones_bf], [pbk])
                ACT(rstd[:, 0:MEM], pbk[:, 0:MEM], AF.Sqrt, [pbk, epsA], [rstd], bias=epsA[:, 0:1], scale=1.0 / D)
                RECIP(rstd[:, 0:MEM], rstd[:, 0:MEM], [rstd], [rstd])
                for c in range(8):
                    STT("dve", memn_t[:, c, :], memT32[:, c, :], vcol(V_MEMKV + c), rstd[:, 0:MEM], ALU.mult, ALU.mult,
                        mt + [rstd, vecs], [memn[c]])
                s, w = wload("memk")
                for c in range(4):
                    pbk = nb()
                    for k in range(8):
                        MM(pbk[:, 0:MEM], w[:, k, c * 128:(c + 1) * 128], memn_t[:, k, :], k == 0, k == 7,
                           [s, memn[k]], [pbk])
                    CP("act", kmT[:, c, :], pbk[:, 0:MEM], [pbk], [kmT])
                s, w = wload("memv")
                for mb in range(2):
                    pbk = nb()
                    for k in range(8):
                        MM(pbk[:], memn_t[:, k, mb * 128:(mb + 1) * 128], w[:, k, :], k == 0, k == 7,
                           [s, memn[k]], [pbk])
                    CP("act", vmT[:, mb, :], pbk[:], [pbk], [vmT])
            MSET("pool", S32[:], 0.0, [S32])
            MSET("pool", S16[:], 0.0, [S16])

            for it in range(NT):
                if ntiles_limit is not None and tile_count >= ntiles_limit:
                    break
                tile_count += 1
                t0 = it * T
                qb0 = it * 4
                first = (it == 0)
                dd = dbg and tile_count == 1
                for hf in range(2):
                    P.dma("sp", [lambda e, b=b, hf=hf, t0=t0: e.dma_start(
                        out=stg[:], in_=x_d[b, t0 + hf * 256:t0 + hf * 256 + 256, :].rearrange("(n p) f -> p n f", p=128))],
                        "stg", writes=[stg])
                    for n in range(2):
                        tb = hf * 2 + n
                        for g in range(2):
                            pbk = nb()
                            for cc in range(4):
                                c = g * 4 + cc
                                TR(pbk[:, cc * 128:(cc + 1) * 128], stg[:, n, c * 128:(c + 1) * 128], ident[:],
                                   [stg, ident], [pbk])
                            CP("act" if g == 0 else "dve", h_t[:, g * 4:g * 4 + 4, tb * 128:(tb + 1) * 128],
                               pbk[:].rearrange("p (a t) -> p a t", a=4), [pbk], hT[g * 4:g * 4 + 4])
                if stages >= 1:
                    ffn("f1", V_F1PRE, V_F1POST)
                if dd:
                    dump("h1", h_t[:], hT)
                if stages >= 2:
                    BAR()
                    rmsnorm_to_xn(V_MIXPRE)
                    s, w = wload("mix_q")
                    for c in range(4):
                        pbk = nb()
                        for k in range(8):
                            MM(pbk[:], w[:, k, c * 128:(c + 1) * 128], xn_t[:, k, :], k == 0, k == 7, [s, xnT[k]], [pbk])
                        P.op("act", lambda e, c=c, pbk=pbk: e.mul(out=QTt[:, c, :], in_=pbk[:], mul=0.125), [pbk], [QT[c]])
                    s, w = wload("mix_k")
                    for c in range(4):
                        pbk = nb()
                        for k in range(8):
                            MM(pbk[:], w[:, k, c * 128:(c + 1) * 128], xn_t[:, k, :], k == 0, k == 7, [s, xnT[k]], [pbk])
                        CP("dve", KTt[:, c, t0:t0 + T], pbk[:], [pbk], [KT[c]])
                    s, w = wload("mix_v")
                    for tb in range(4):
                        pbk = nb()
                        for k in range(8):
                            MM(pbk[:], xn_t[:, k, tb * 128:(tb + 1) * 128], w[:, k, :], k == 0, k == 7, [s, xnT[k]], [pbk])
                        CP("act", Vt[:, qb0 + tb, :], pbk[:], [pbk], [VT[qb0 + tb]])
                    for hd in range(8):
                        fc = hd // 2
                        po = (hd % 2) * 64
                        acc = accb[hd % 2]
                        Oacc = banks[6 + hd % 2]
                        MSET("pool", acc[:], 0.0, [acc])
                        kbs = list(range(qb0 + 3, -1, -1))
                        for idx, kb in enumerate(kbs):
                            E = Ebuf[idx % 2]; SP = SPb[idx % 2]; t1 = t1b[idx % 2]; A = Ab[idx % 2]
                            pz = nb()
                            MM(pz[:], KTt[po:po + 64, fc, kb * 128:(kb + 1) * 128], QTt[po:po + 64, fc, :], True, True,
                               [KT[fc], QT[fc]], [pz])
                            ACT(E[:], pz[:], AF.Exp, [pz], [E])
                            ACT(SP[:], E[:], AF.Ln, [E, epsA], [SP], bias=epsA[:, 3:4])
                            diag = kb >= qb0
                            if diag:
                                r = kb - qb0
                                TT("pool", SP[:], SP[:], mask4[:, r, :], ALU.mult, [SP, mask4], [SP])
                            pw = nb()
                            MM(pw[:], triU_bf[:], SP[:], True, True, [triU_bf, SP], [pw])
                            pc = nb()
                            MM(pc[:], ones_bf[:], SP[:], True, True, [ones_bf, SP], [pc])
                            TT("dve", t1[:], pz[:], SP[:], ALU.subtract, [pz, SP], [t1])
                            TT("dve", t1[:], t1[:], pw[:], ALU.subtract, [t1, pw], [t1])
                            TT("pool", t1[:], t1[:], acc[:], ALU.subtract, [t1, acc], [t1])
                            TT("dve", acc[:], acc[:], pc[:], ALU.add, [acc, pc], [acc])
                            ACT(A[:], t1[:], AF.Exp, [t1], [A])
                            if diag:
                                TT("pool", A[:], A[:], mask4[:, r, :], ALU.mult, [A, mask4], [A])
                            MM(Oacc[:], Vt[:, kb, fc * 128:(fc + 1) * 128], A[:], idx == 0, idx == len(kbs) - 1,
                               [VT[kb], A], [Oacc])
                        CP("act", Oraw_t[po:po + 64, fc, :], Oacc[po:po + 64, :], [Oacc], [Oraw[fc]])
                    for fc in range(4):
                        q = sqb[sqrot[0] % 2]
                        sqrot[0] += 1
                        ACT(q[:], Oraw_t[:, fc, :], AF.Square, [Oraw[fc]], [q])
                        pbk = nb()
                        MM(pbk[:], blk_bf[:], q[:], True, True, [blk_bf, q], [pbk])
                        ACT(rstd[:], pbk[:], AF.Sqrt, [pbk, epsA], [rstd], bias=epsA[:, 0:1], scale=1.0 / 64)
                        RECIP(rstd[:], rstd[:], [rstd], [rstd])
                        STT("dve", mixcat_t[:, fc, :], Oraw_t[:, fc, :], vcol(V_SBG + fc), rstd[:], ALU.mult, ALU.mult,
                            [Oraw[fc], rstd, vecs], [mixcat[fc]])

                    P.mute = stages < 3
                    BAR()
                    rawi = [0]

                    def lerp(ps_ap, ps_tile, M, ci, mucol, dest, dest_tile):
                        raw = rawb[rawi[0] % 2]
                        rawi[0] += 1
                        if first:
                            MSET("pool", raw[0:M, 0:1], 0.0, [raw])
                        else:
                            CP("pool", raw[0:M, 0:1], carry[0:M, ci:ci + 1], [carry], [raw])
                        CP("act", raw[0:M, 1:T + 1], ps_ap, [ps_tile], [raw])
                        CP("pool", carry[0:M, ci:ci + 1], raw[0:M, T:T + 1], [raw], [carry])
                        TT("dve", dtmp[0:M, :], raw[0:M, 0:T], raw[0:M, 1:T + 1], ALU.subtract, [raw], [dtmp])
                        STT("dve", dest, dtmp[0:M, :], vecs[0:M, mucol:mucol + 1], raw[0:M, 1:T + 1], ALU.mult, ALU.add,
                            [dtmp, raw, vecs], [dest_tile])

                    s, w = wload("mix_lora")
                    pbk = nb()
                    for k in range(8):
                        MM(pbk[:], w[:, k, 0:128], xn_t[:, k, :], k == 0, k == 7, [s, xnT[k]], [pbk])
                    lerp(pbk[:], pbk, 128, 12, V_MU_XWA, xwa32[:], xwa32)
                    ACT(txa[0:64, :], xwa32[0:64, :], AF.Tanh, [xwa32], [txa])
                    CP("pool", txa[64:128, :], xwa32[64:128, :], [xwa32], [txa])
                    pbk = nb()
                    for k in range(8):
                        MM(pbk[:], w[:, k, 128:256], xn_t[:, k, :], k == 0, k == 7, [s, xnT[k]], [pbk])
                    lerp(pbk[:], pbk, 128, 13, V_MU_XG0, xwa32[:], xwa32)
                    ACT(sxg[:, 0, :], xwa32[:], AF.Sigmoid, [xwa32], [sxg])
                    pbk = nb()
                    for k in range(8):
                        MM(pbk[0:32, :], w[:, k, 256:288], xn_t[:, k, :], k == 0, k == 7, [s, xnT[k]], [pbk])
                    lerp(pbk[0:32, :], pbk, 32, 14, V_MU_XG1, xwa32[0:32, :], xwa32)
                    ACT(sxg[0:32, 1, :], xwa32[0:32, :], AF.Sigmoid, [xwa32], [sxg])

                    for fc in range(4):
                        P.mute = stages < 3 or rwsub < 2
                        s, w = wload("mix_rw%d" % fc)
                        for which, dest, dtile, ci, mucol in ((0, r32[:], r32, fc, V_MU_R + fc),
                                                              (1, k32[:], k32, 4 + fc, V_MU_K + fc),
                                                              (2, vTb[:], vTb, 8 + fc, V_MU_V + fc)):
                            pbk = nb()
                            for k in range(8):
                                MM(pbk[:], w[:, k, which * 128:(which + 1) * 128], xn_t[:, k, :], k == 0, k == 7,
                                   [s, xnT[k]], [pbk])
                            lerp(pbk[:], pbk, 128, ci, mucol, dest, dtile)
                        cs = slice(fc * 128, (fc + 1) * 128)
                        pbk = nb()
                        MM(pbk[:], smallw[0:64, 0, cs], txa[0:64, :], True, True, [smallw, txa], [pbk])
                        ACT(sig[:], pbk[:], AF.Sigmoid, [pbk, vecs], [sig], bias=vcol(V_W0 + fc))
                        P.op("dve", lambda e: e.tensor_tensor_scan(out=cum[:], data0=resetm[:], data1=sig[:], initial=0.0,
                                                                   op0=ALU.mult, op1=ALU.add), [resetm, sig], [cum])
                        TT("pool", cumx[:], cum[:], sig[:], ALU.subtract, [cum, sig], [sig])
                        ACT(eW[:], cum[:], AF.Exp, [cum], [eW], scale=-CDEC)
                        ACT(eWx[:], cumx[:], AF.Exp, [cumx], [eWx], scale=-CDEC)
                        ACT(eWi[:], cum[:], AF.Exp, [cum], [eWi], scale=CDEC)
                        CP("pool", WC[:, 0:4], eW[:].rearrange("p (c t) -> p c t", t=128)[:, :, 127], [eW], [WC])
                        pbk = nb()
                        MM(pbk[:], smallw[64:128, 0, cs], txa[64:128, :], True, True, [smallw, txa], [pbk])
                        ACT(a32[:], pbk[:], AF.Sigmoid, [pbk, vecs], [a32], bias=vcol(V_A0 + fc))
                        TS1("dve", kk32[:], k32[:], vcol(V_KK + fc), ALU.mult, [k32, vecs], [kk32])
                        q = sqb[sqrot[0] % 2]
                        sqrot[0] += 1
                        ACT(q[:], kk32[:], AF.Square, [kk32], [q])
                        pbk = nb()
                        MM(pbk[:], blk_bf[:], q[:], True, True, [blk_bf, q], [pbk])
                        ACT(nrm[:], pbk[:], AF.Sqrt, [pbk], [nrm])
                        TS1("dve", nrm[:], nrm[:], 1e-12, ALU.max, [nrm], [nrm])
                        RECIP(nrm[:], nrm[:], [nrm], [nrm])
                        TT("dve", kkn[:], kk32[:], nrm[:], ALU.mult, [kk32, nrm], [kkn])
                        TS("pool", kmod[:], a32[:], vcol(V_KA + fc), vcol(V_KA + fc), ALU.mult, ALU.subtract, [a32, vecs], [kmod])
                        STT("pool", kmod[:], kmod[:], 1.0, k32[:], ALU.add, ALU.mult, [kmod, k32], [kmod])
                        ar4 = ARt[:]
                        STT("dve", ar4[:, :, 0:128], kkn[:].rearrange("p (c t) -> p c t", t=128), -1.0,
                            eWx[:].rearrange("p (c t) -> p c t", t=128), ALU.mult, ALU.mult, [kkn, eWx], [ARt])
                        TT("pool", ar4[:, :, 128:256], r32[:].rearrange("p (c t) -> p c t", t=128),
                           eW[:].rearrange("p (c t) -> p c t", t=128), ALU.mult, [r32, eW], [ARt])
                        TT("dve", b32[:], kkn[:], a32[:], ALU.mult, [kkn, a32], [b32])
                        TT("dve", BTt[:], b32[:], eWi[:], ALU.mult, [b32, eWi], [BTt])
                        TT("pool", KTl[:], kmod[:], eWi[:], ALU.mult, [kmod, eWi], [KTl])
                        STT("dve", rkr_t[:, fc, :], r32[:], vcol(V_RK + fc), kmod[:], ALU.mult, ALU.mult,
                            [r32, kmod, vecs], [rkr[fc]])
                        P.mute = stages < 3 or rwsub < 3
                        for src, stile, dst_ap, dtile in ((BTt, BTt, Btok[:], Btok), (KTl, KTl, Ktok[:], Ktok),
                                                          (vTb, vTb, Vtok_t[:, :, cs], Vtok[fc])):
                            pbk = nb()
                            pbb = pbk[:].bitcast(BF16)
                            for tc in range(4):
                                TR(pbb[:, tc * 128:(tc + 1) * 128], src[:, tc * 128:(tc + 1) * 128], identb[:],
                                   [stile, identb], [pbk])
                            CP("act", dst_ap, pbb[:, 0:512].rearrange("p (c f) -> p c f", f=128), [pbk], [dtile])
                        P.mute = stages < 3 or rwsub < 4
                        for tc in range(4):
                            tcs = slice(tc * 128, (tc + 1) * 128)
                            pm = nb(); pk = nb(); pn = nb()
                            for hh in range(2):
                                po = hh * 64
                                MM(pm[:, hh * 256:(hh + 1) * 256], BTt[po:po + 64, tcs], ARt[po:po + 64, tc, :], True, True,
                                   [BTt, ARt], [pm])
                                MM(pk[:, hh * 256:(hh + 1) * 256], KTl[po:po + 64, tcs], ARt[po:po + 64, tc, :], True, True,
                                   [KTl, ARt], [pk])
                                MM(pn[:, hh * 128:(hh + 1) * 128], ARt[po:po + 64, tc, 0:128], BTt[po:po + 64, tcs], True, True,
                                   [ARt, BTt], [pn])
                            TT("dve", MMb[:].rearrange("p a b -> p (a b)"), pm[:], maskAB[:], ALU.mult, [pm, maskAB], [MMb])
                            TT("dve", KMb[:].rearrange("p a b -> p (a b)"), pk[:], maskAB[:], ALU.mult, [pk, maskAB], [KMb])
                            X = Xb[0]; XT = XTb[0]
                            TT("dve", X[:].rearrange("p a b -> p (a b)"), pn[:, 0:256], maskN2[:], ALU.mult, [pn, maskN2], [X])
                            CP("pool", XT[:], MMb[:, :, 0:128], [MMb], [XT])
                            pr = nb()
                            MM(pr[:, 0:128], ARt[:, tc, 0:128], S16[:, fc, :], True, False, [ARt, S16], [pr])
                            for hh in range(2):
                                MM(pr[:, hh * 64:(hh + 1) * 64], KMb[:, hh, 0:128],
                                   Vtok_t[:, tc, fc * 128 + hh * 64:fc * 128 + hh * 64 + 64], False, hh == 1,
                                   [KMb, Vtok[fc]], [pr])
                            CP("dve", SA32[:], pr[:, 0:128], [pr], [SA32])
                            CP("act", SA16[:], SA32[:], [SA32], [SA16])
                            for i in range(7):
                                X = Xb[i % 2]; XT = XTb[i % 2]
                                pd = nb()
                                for hh in range(2):
                                    MM(pd[:, hh * 64:(hh + 1) * 64], XT[:, hh, :], SA16[:, hh * 64:(hh + 1) * 64], True, True,
                                       [XT, SA16], [pd])
                                TT("dve", SA32[:], SA32[:], pd[:, 0:128], ALU.add, [SA32, pd], [SA32])
                                CP("act", SA16[:], SA32[:], [SA32], [SA16])
                                if i < 6:
                                    Xn = Xb[(i + 1) % 2]; XTn = XTb[(i + 1) % 2]
                                    px = nb(); pxt = nb()
                                    for hh in range(2):
                                        MM(px[:, hh * 128:(hh + 1) * 128], XT[:, hh, :], X[:, hh, :], True, True, [XT, X], [px])
                                        MM(pxt[:, hh * 128:(hh + 1) * 128], X[:, hh, :], XT[:, hh, :], True, True, [XT, X], [pxt])
                                    CP("pool" if False else "dve", Xn[:].rearrange("p a b -> p (a b)"), px[:, 0:256], [px], [Xn])
                                    CP("act", XTn[:].rearrange("p a b -> p (a b)"), pxt[:, 0:256], [pxt], [XTn])
                            py = nb()
                            MM(py[:, 0:128], ARt[:, tc, 128:256], S16[:, fc, :], True, False, [ARt, S16], [py])
                            for hh in range(2):
                                hs = slice(hh * 64, (hh + 1) * 64)
                                MM(py[:, hs], MMb[:, hh, 128:256], SA16[:, hs], False, False, [MMb, SA16], [py])
                                MM(py[:, hs], KMb[:, hh, 128:256], Vtok_t[:, tc, fc * 128 + hh * 64:fc * 128 + hh * 64 + 64],
                                   False, hh == 1, [KMb, Vtok[fc]], [py])
                            CP("act", Yall_t[:, tc, cs], py[:, 0:128], [py], [Yall[tc]])
                            pst = nb()
                            MM(pst[:, 0:128], Btok[:, tc, :], SA16[:], True, False, [Btok, SA16], [pst])
                            MM(pst[:, 0:128], Ktok[:, tc, :], Vtok_t[:, tc, cs], False, True, [Ktok, Vtok[fc]], [pst])
                            for hh in range(2):
                                ps_ = slice(hh * 64, (hh + 1) * 64)
                                TS1("pool", S32[ps_, fc, ps_], S32[ps_, fc, ps_], WC[ps_, tc:tc + 1], ALU.mult, [S32, WC], [S32])
                                STT("dve", S32[ps_, fc, ps_], pst[ps_, ps_], WC[ps_, tc:tc + 1], S32[ps_, fc, ps_],
                                    ALU.mult, ALU.add, [pst, WC, S32], [S32])
                            CP("act", S16[:, fc, :], S32[:, fc, :], [S32], [S16])
                    P.mute = stages < 3 or rwsub < 5
                    for tc in range(4):
                        tcs = slice(tc * 128, (tc + 1) * 128)
                        pg = nb()
                        MM(pg[:], sxg[:, 0, tcs], smallw[:, 1, :], True, False, [sxg, smallw], [pg])
                        MM(pg[:], sxg[0:32, 1, tcs], smallw[0:32, 2, :], False, True, [sxg, smallw], [pg])
                        pbn = nb()
                        for fc in range(4):
                            MM(pbn[:, fc * 2:fc * 2 + 2], rkr_t[:, fc, tcs], ind2[:], True, True, [rkr[fc], ind2], [pbn])
                        CP("act", st8[:, 5, :], pbn[:, 0:8], [pbn], [st8])
                        Y = Yall_t[:, tc, :]
                        Y3 = Y.rearrange("p (h d) -> p h d", d=64)
                        P.op("dve", lambda e, Y3=Y3: e.tensor_reduce(out=st8[:, 0, :], in_=Y3, axis=AX.X, op=ALU.add),
                             [Yall[tc]], [st8])
                        ACT(ysq[:], Y, AF.Square, [Yall[tc]], [ysq])
                        P.op("dve", lambda e: e.tensor_reduce(out=st8[:, 1, :], in_=ysq[:].rearrange("p (h d) -> p h d", d=64),
                                                              axis=AX.X, op=ALU.add), [ysq], [st8])
                        TS1("dve", st8[:, 2, :], st8[:, 0, :], 1.0 / 64, ALU.mult, [st8], [st8])
                        TT("dve", st8[:, 3, :], st8[:, 2, :], st8[:, 2, :], ALU.mult, [st8], [st8])
                        STT("dve", st8[:, 3, :], st8[:, 1, :], 1.0 / 64, st8[:, 3, :], ALU.mult, ALU.subtract, [st8], [st8])
                        ACT(st8[:, 4, :], st8[:, 3, :], AF.Sqrt, [st8, epsA], [st8], bias=epsA[:, 2:3])
                        RECIP(st8[:, 4, :], st8[:, 4, :], [st8], [st8])
                        for hd in range(8):
                            hs = slice(hd * 64, (hd + 1) * 64)
                            TS("dve" if hd % 2 == 0 else "pool", yn[:, hs], Yall_t[:, tc, hs], st8[:, 2, hd:hd + 1],
                               st8[:, 4, hd:hd + 1], ALU.subtract, ALU.mult, [Yall[tc], st8], [yn])
                        TT("dve", yn[:], yn[:], bvecs[:, 0:512], ALU.mult, [yn, bvecs], [yn])
                        TT("pool", yn[:], yn[:], bvecs[:, 512:1024], ALU.add, [yn, bvecs], [yn])
                        for hd in range(8):
                            hs = slice(hd * 64, (hd + 1) * 64)
                            STT("dve" if hd % 2 == 0 else "pool", yn[:, hs], Vtok_t[:, tc, hs], st8[:, 5, hd:hd + 1], yn[:, hs],
                                ALU.mult, ALU.add, [Vtok[hd // 2], st8, yn], [yn])
                        TT("dve", rwo[:], pg[:], yn[:], ALU.mult, [pg, yn], [rwo])
                        pbk = nb()
                        pbb = pbk[:].bitcast(BF16)
                        for fc in range(4):
                            TR(pbb[:, fc * 128:(fc + 1) * 128], rwo[:, fc * 128:(fc + 1) * 128], identb[:], [rwo, identb], [pbk])
                        CP("act", mixcat_t[:, 4:8, tcs], pbb[:, 0:512].rearrange("p (c t) -> p c t", t=128), [pbk], mixcat[4:8])
                    P.mute = False
                    if dd:
                        dump("sbo", mixcat_t[:], mixcat)
                    for g in range(2):
                        s, w = wload("mixo%d" % g)
                        for cc in range(4):
                            c = g * 4 + cc
                            pbk = nb()
                            for k in range(8):
                                MM(pbk[:], w[:, k, cc * 128:(cc + 1) * 128], mixcat_t[:, k, :], k == 0, k == 7,
                                   [s, mixcat[k]], [pbk])
                            CP("act", tmp8[:, c, :], pbk[:], [pbk], [tmpT[c]])
                    post_residual(V_MIXPOST, False)
                if dd:
                    dump("h2", h_t[:], hT)
                if stages >= 4:
                    BAR()
                    rmsnorm_to_xn(V_MEMPRE)
                    s, w = wload("memq")
                    for c in range(4):
                        pbk = nb()
                        for k in range(8):
                            MM(pbk[:], w[:, k, c * 128:(c + 1) * 128], xn_t[:, k, :], k == 0, k == 7, [s, xnT[k]], [pbk])
                        P.op("act", lambda e, c=c, pbk=pbk: e.mul(out=qTm_t[:, c, :], in_=pbk[:], mul=128.0 ** -0.5),
                             [pbk], [qTm[c]])
                    for hd in range(4):
                        es = []
                        for mb in range(2):
                            pbk = nb()
                            MM(pbk[:], kmT[:, hd, mb * 128:(mb + 1) * 128], qTm_t[:, hd, :], True, True, [kmT, qTm[hd]], [pbk])
                            Et = Em[(hd % 2) * 2 + mb]
                            ACT(Et[:], pbk[:], AF.Exp, [pbk], [Et])
                            es.append(Et)
                        pS = nb()
                        MM(pS[:], ones_bf[:], es[0][:], True, False, [ones_bf, es[0]], [pS])
                        MM(pS[:], ones_bf[:], es[1][:], False, True, [ones_bf, es[1]], [pS])
                        pO = nb()
                        MM(pO[:], vmT[:, 0, hd * 128:(hd + 1) * 128], es[0][:], True, False, [vmT, es[0]], [pO])
                        MM(pO[:], vmT[:, 1, hd * 128:(hd + 1) * 128], es[1][:], False, True, [vmT, es[1]], [pO])
                        RECIP(rsb[:], pS[:], [pS], [rsb])
                        TT("dve", oTm_t[:, hd, :], pO[:], rsb[:], ALU.mult, [pO, rsb], [oTm[hd]])
                    s, w = wload("memo")
                    for c in range(8):
                        pbk = nb()
                        for k in range(4):
                            MM(pbk[:], w[:, k, c * 128:(c + 1) * 128], oTm_t[:, k, :], k == 0, k == 3, [s, oTm[k]], [pbk])
                        CP("act", tmp8[:, c, :], pbk[:], [pbk], [tmpT[c]])
                    post_residual(V_MEMPOST, False)
                if dd:
                    dump("h3", h_t[:], hT)
                if stages >= 5:
                    ffn("f2", V_F2PRE, V_F2POST)
                for tb in range(4):
                    for g in range(2):
                        pbk = nb()
                        for cc in range(4):
                            c = g * 4 + cc
                            TR(pbk[:, cc * 128:(cc + 1) * 128], h_t[:, c, tb * 128:(tb + 1) * 128], ident[:],
                               [hT[c], ident], [pbk])
                        CP("act" if g == 0 else "dve", ost[:, tb, g * 512:(g + 1) * 512], pbk[:], [pbk],
                           [tmpT[2 * tb], tmpT[2 * tb + 1]])
                P.dma("sp", [lambda e, b=b, t0=t0: e.dma_start(
                    out=out_d[b, t0:t0 + T, :].rearrange("(n p) f -> p n f", p=128), in_=ost)], "ost",
                    reads=tmpT, final=True)
        P.emit()
    return nc


_CACHE = {}


def kernel(**inputs):
    inp = {k: np.asarray(v) for k, v in inputs.items()}
    wf = _host_weights(inp)
    vecs = _host_vecs(inp)
    small = _host_small(inp)
    bv = _host_bvecs(inp)
    if "nc" not in _CACHE:
        _CACHE["nc"] = build_nc()
    nc = _CACHE["nc"]
    in_maps = []
    for c in range(8):
        in_maps.append({
            "x": np.ascontiguousarray(inp["x"][c * NB:(c + 1) * NB]),
            "mem": np.ascontiguousarray(inp["mem"][c * NB:(c + 1) * NB]),
            "wf": wf, "vecs": vecs, "small": small, "bvecs": bv,
        })
    res = run_bass_kernel_spmd(nc, in_maps, core_ids=list(range(8)))
    out = np.concatenate([np.asarray(r["out"]) for r in res.results], axis=0)
    return out.astype(np.float32)
```

```python
import contextlib
import math
import numpy as np
import concourse.bass as bass
import concourse.mybir as mybir
from concourse.bass_utils import run_bass_kernel_spmd

F32 = mybir.dt.float32
BF16 = mybir.dt.bfloat16
AF = mybir.ActivationFunctionType
ALU = mybir.AluOpType
AX = mybir.AxisListType

SAME_ENGINE_SYNC = ('act', 'dve', 'pool')
ENGS = ("pe", "act", "dve", "pool", "sp")

NB = 2
SEQ = 2048
D = 1024
T = 512
NT = SEQ // T
DFF = 2816
NJ = DFF // 128
MEM = 256
CDEC = math.exp(-0.5)
NSLOT = 3
SLOT_ELEMS = 4096


class Tile:
    __slots__ = ("t", "name", "last_w", "readers", "excl")

    def __init__(self, t, name="", excl=False):
        self.t = t
        self.name = name
        self.last_w = None
        self.readers = []
        self.excl = excl

    def __getitem__(self, k):
        return self.t[k]


class _Op:
    __slots__ = ("eng", "fn", "deps", "waits", "signal", "sigval", "dma", "idx", "force")


class Prog:
    def __init__(self, nc):
        self.nc = nc
        self.ops = {e: [] for e in ENGS}
        self.dma_sems = {}
        self.dma_tok = []
        self.final_tokens = []

    def _deps_for(self, reads, writes):
        deps = []
        for t in reads:
            if t.last_w is not None:
                deps.append(t.last_w)
            if t.excl:
                deps.extend(t.readers)
        for t in writes:
            if t.last_w is not None:
                deps.append(t.last_w)
            deps.extend(t.readers)
        return deps

    def _commit(self, tok, reads, writes):
        for t in reads:
            t.readers.append(tok)
        for t in writes:
            t.last_w = tok
            t.readers = []

    def _new(self, eng, fn, reads, writes):
        o = _Op()
        o.eng = eng
        o.fn = fn
        o.deps = self._deps_for(reads, writes)
        o.waits = []
        o.signal = False
        o.sigval = 0
        o.dma = None
        o.force = -1
        o.idx = len(self.ops[eng])
        self.ops[eng].append(o)
        return o

    mute = False

    def op(self, eng, fn, reads=(), writes=(), force_prev=False):
        if self.mute:
            return None
        o = self._new(eng, fn, reads, writes)
        if force_prev and o.idx > 0:
            for j in range(o.idx - 1, -1, -1):
                pj = self.ops[eng][j]
                if pj.fn is not None and pj.dma is None:
                    o.force = j
                    o.deps.append(("e", eng, j))
                    break
        tok = ("e", eng, o.idx)
        self._commit(tok, reads, writes)
        return tok

    def dma(self, eng, fns, semkey, reads=(), writes=(), final=False):
        if self.mute:
            return None
        o = self._new(eng, fns, reads, writes)
        ent = self.dma_sems.setdefault(semkey, [len(self.dma_sems), 0])
        ent[1] += 16 * len(fns)
        did = len(self.dma_tok)
        self.dma_tok.append((semkey, ent[1]))
        o.dma = semkey
        tok = ("d", did)
        self._commit(tok, reads, writes)
        if final:
            self.final_tokens.append(tok)
        return tok

    def barrier(self, extra=()):
        engs = ("pe", "act", "dve", "pool")
        deps = list(extra)
        for Pn in engs:
            for o in reversed(self.ops[Pn]):
                if o.fn is not None and o.dma is None:
                    deps.append(("e", Pn, o.idx))
                    break
        for E in engs:
            o = self._new(E, None, (), ())
            o.deps = list(deps)

    def emit(self, final_eng="sp"):
        nc = self.nc
        fo = self._new(final_eng, None, (), ())
        fo.deps = list(self.final_tokens)
        for E in ENGS:
            waited = {}
            for o in self.ops[E]:
                need = {}
                for tok in o.deps:
                    if tok[0] == "e":
                        _, Pn, idx = tok
                        if Pn == E and ((E not in SAME_ENGINE_SYNC and idx != o.force) or idx >= o.idx):
                            continue
                        key = ("e", Pn)
                        if need.get(key, (-1, None))[0] < idx:
                            need[key] = (idx, tok)
                    else:
                        semkey, val = self.dma_tok[tok[1]]
                        key = ("d", semkey)
                        if need.get(key, (-1, None))[0] < val:
                            need[key] = (val, tok)
                for key, (v, tok) in need.items():
                    if waited.get(key, -1) >= v:
                        continue
                    waited[key] = v
                    if tok[0] == "e":
                        self.ops[tok[1]][tok[2]].signal = True
                    o.waits.append(tok)
        for E in ENGS:
            c = 0
            for o in self.ops[E]:
                if o.signal:
                    c += 1
                o.sigval = c
        with contextlib.ExitStack() as st:
            esem = {E: st.enter_context(nc.semaphore("s_" + E)) for E in ENGS}
            dsem = {}
            for k, (i, _) in self.dma_sems.items():
                dsem[k] = st.enter_context(nc.semaphore("d%d" % i))
            block = st.enter_context(nc.Block())

            def run(E, eng):
                for o in self.ops[E]:
                    for tok in o.waits:
                        if tok[0] == "e":
                            _, Pn, idx = tok
                            eng.wait_ge(esem[Pn], self.ops[Pn][idx].sigval)
                        else:
                            semkey, val = self.dma_tok[tok[1]]
                            eng.wait_ge(dsem[semkey], val)
                    if o.fn is None:
                        continue
                    if o.dma is not None:
                        for f in o.fn:
                            f(eng).then_inc(dsem[o.dma], 16)
                    else:
                        ins = o.fn(eng)
                        if o.signal:
                            ins.then_inc(esem[E], 1)

            @block.tensor
            def _(eng):
                run("pe", eng)

            @block.scalar
            def _(eng):
                run("act", eng)

            @block.vector
            def _(eng):
                run("dve", eng)

            @block.gpsimd
            def _(eng):
                run("pool", eng)

            @block.sync
            def _(eng):
                run("sp", eng)


RW0 = 1536


def _piece(W, cols):
    K = W.shape[0]
    kc = K // 128
    sub = W[:, cols]
    return np.ascontiguousarray(sub.reshape(kc, 128, len(cols)).transpose(1, 0, 2))


def _plan():
    pl = []
    for f in ("f1", "f2"):
        for i in range(NJ // 2):
            pl.append((f + "_in%d" % i, 8, 512))
        for c in range(8):
            pl.append((f + "_out%d" % c, NJ, 128))
    pl += [("mix_q", 8, 512), ("mix_k", 8, 512), ("mix_v", 8, 512), ("mix_lora", 8, 288)]
    for fc in range(4):
        pl.append(("mix_rw%d" % fc, 8, 384))
    pl += [("mixo0", 8, 512), ("mixo1", 8, 512), ("memq", 8, 512), ("memo", 4, 1024),
           ("memk", 8, 512), ("memv", 8, 512)]
    return pl


def _offsets():
    off = {}
    o = 0
    for name, kc, gc in _plan():
        off[name] = (o, kc, gc)
        o += kc * gc
    return off, o


def _host_weights(inp):
    ar = np.arange
    pieces = {}
    for f, wi, wo in (("f1", "ffn1_w_in", "ffn1_w_out"), ("f2", "ffn2_w_in", "ffn2_w_out")):
        Wi = inp[wi][0]
        Wo = inp[wo][0]
        for i in range(NJ // 2):
            cols = np.concatenate([ar(128) + (2 * i) * 128, ar(128) + DFF + (2 * i) * 128,
                                   ar(128) + (2 * i + 1) * 128, ar(128) + DFF + (2 * i + 1) * 128])
            pieces[f + "_in%d" % i] = _piece(Wi, cols)
        for c in range(8):
            pieces[f + "_out%d" % c] = _piece(Wo, ar(128) + c * 128)
    Wm = inp["mix_w_in"][0]
    pieces["mix_q"] = _piece(Wm, ar(512))
    pieces["mix_k"] = _piece(Wm, ar(512) + 512)
    pieces["mix_v"] = _piece(Wm, ar(512) + 1024)
    pieces["mix_lora"] = _piece(Wm, ar(288) + RW0 + 1536)
    for fc in range(4):
        cols = np.concatenate([ar(128) + RW0 + fc * 128, ar(128) + RW0 + 512 + fc * 128,
                               ar(128) + RW0 + 1024 + fc * 128])
        pieces["mix_rw%d" % fc] = _piece(Wm, cols)
    Wo = inp["mix_w_out"][0]
    pieces["mixo0"] = _piece(Wo, ar(512))
    pieces["mixo1"] = _piece(Wo, ar(512) + 512)
    pieces["memq"] = _piece(inp["mem_w_q"][0], ar(512))
    pieces["memo"] = _piece(inp["mem_w_o"][0], ar(1024))
    pieces["memk"] = _piece(inp["mem_w_kv"][0], ar(512))
    pieces["memv"] = _piece(inp["mem_w_kv"][0], ar(512) + 512)
    off, total = _offsets()
    wf = np.empty((128, total), np.float32)
    for name, (o, kc, gc) in off.items():
        wf[:, o:o + kc * gc] = pieces[name].reshape(128, kc * gc)
    return wf


V_F1PRE, V_F1POST, V_MIXPRE, V_MIXPOST, V_MEMPRE, V_MEMPOST, V_F2PRE, V_F2POST = [8 * i for i in range(8)]
V_MU_R, V_MU_K, V_MU_V = 64, 68, 72
V_MU_XWA, V_MU_XG0, V_MU_XG1 = 76, 77, 78
V_W0, V_A0, V_KK, V_KA, V_RK, V_SBG = 79, 83, 87, 91, 95, 99
V_MEMKV = 103
NV = 112


def _host_vecs(inp):
    v = np.zeros((128, NV), np.float32)

    def put(col, vec):
        n = vec.shape[0]
        nch = (n + 127) // 128
        for c in range(nch):
            seg = vec[c * 128:(c + 1) * 128]
            v[:seg.shape[0], col + c] = seg

    put(V_F1PRE, inp["ffn1_pre"][0]); put(V_F1POST, inp["ffn1_post"][0])
    put(V_MIXPRE, inp["mix_pre"][0]); put(V_MIXPOST, inp["mix_post"][0])
    put(V_MEMPRE, inp["mem_pre"][0]); put(V_MEMPOST, inp["mem_post"][0])
    put(V_F2PRE, inp["ffn2_pre"][0]); put(V_F2POST, inp["ffn2_post"][0])
    mu = inp["rwkv_mu"][0]
    put(V_MU_R, mu[0:512]); put(V_MU_K, mu[512:1024]); put(V_MU_V, mu[1024:1536])
    put(V_MU_XWA, mu[1536:1664]); put(V_MU_XG0, mu[1664:1792]); put(V_MU_XG1, mu[1792:1824])
    put(V_W0, inp["rwkv_w0"][0]); put(V_A0, inp["rwkv_a0"][0])
    put(V_KK, inp["rwkv_k_k"][0]); put(V_KA, inp["rwkv_k_a"][0])
    put(V_RK, inp["rwkv_r_k"][0].reshape(-1)); put(V_SBG, inp["sb_out_g"][0])
    put(V_MEMKV, inp["mem_kv_g"][0])
    return v


def _host_small(inp):
    s = np.zeros((128, 3, 512), np.float32)
    s[0:64, 0] = inp["rwkv_w2"][0]
    s[64:128, 0] = inp["rwkv_a2"][0]
    g2 = inp["rwkv_g2"][0]
    s[:, 1] = g2[0:128]
    s[0:32, 2] = g2[128:160]
    return s


def _host_bvecs(inp):
    b = np.empty((128, 1024), np.float32)
    b[:, 0:512] = inp["rwkv_lnx_g"][0][None, :]
    b[:, 512:1024] = inp["rwkv_lnx_b"][0][None, :]
    return b


def build_nc(ntiles_limit=None, dbg=False, stages=99, rwsub=99):
    nc = bass.Bass("TRN2", target_bir_lowering=False)
    off, WTOT = _offsets()
    x_d = nc.dram_tensor("x", [NB, SEQ, D], F32, kind="ExternalInput").ap()
    mem_d = nc.dram_tensor("mem", [NB, MEM, D], F32, kind="ExternalInput").ap()
    wf_d = nc.dram_tensor("wf", [128, WTOT], F32, kind="ExternalInput").ap()
    vecs_d = nc.dram_tensor("vecs", [128, NV], F32, kind="ExternalInput").ap()
    small_d = nc.dram_tensor("small", [128, 3, 512], F32, kind="ExternalInput").ap()
    bvecs_d = nc.dram_tensor("bvecs", [128, 1024], F32, kind="ExternalInput").ap()
    out_d = nc.dram_tensor("out", [NB, SEQ, D], F32, kind="ExternalOutput").ap()
    wb_d = nc.dram_tensor("wbf", [128, WTOT], BF16, kind="Internal").ap()
    dbg_d = {}
    if dbg:
        for nm in ("h1", "q", "sbo", "rwo", "h2", "h3"):
            dbg_d[nm] = nc.dram_tensor("dbg_" + nm, [128, 8, T], F32, kind="ExternalOutput").ap()

    P = Prog(nc)
    with contextlib.ExitStack() as st:
        def sb(name, shape, dt=F32):
            return st.enter_context(nc.sbuf_tensor("sb_" + name, shape, dt))

        def sbt(name, shape, dt=F32):
            return Tile(sb(name, shape, dt), name)

        h_t = sb("h", [128, 8, T])
        hT = [Tile(h_t, "h%d" % c) for c in range(8)]
        xn_t = sb("xn", [128, 8, T], BF16)
        xnT = [Tile(xn_t, "xn%d" % c) for c in range(8)]
        tmp_t = sb("tmp8", [128, 8 * T])
        tmpT = [Tile(tmp_t, "tmp%d" % c) for c in range(8)]
        tmp8 = tmp_t[:].rearrange("p (c t) -> p c t", t=T)
        ost = tmp_t[:].rearrange("p (n f) -> p n f", f=D)
        stg = sbt("stg", [128, 2, D])
        wslot = [sbt("wslot%d" % i, [128, SLOT_ELEMS], BF16) for i in range(NSLOT)]
        KTt = sb("KT", [128, 4, SEQ], BF16)
        KT = [Tile(KTt, "KT%d" % c) for c in range(4)]
        Vt = sb("V", [128, 16, 512], BF16)
        VT = [Tile(Vt, "V%d" % c) for c in range(16)]
        vecs = sbt("vecs", [128, NV])
        bvecs = sbt("bvecs", [128, 1024])
        smallw = sbt("smallw", [128, 3, 512], BF16)
        kmT = sbt("kmT", [128, 4, MEM], BF16)
        vmT = sbt("vm", [128, 2, 512], BF16)
        S32 = sbt("S32", [128, 4, 128])
        S16 = sbt("S16", [128, 4, 128], BF16)
        carry = sbt("carry", [128, 16])
        rstd = sbt("rstd", [128, T])
        sqb = [sbt("sqb%d" % i, [128, T], BF16) for i in range(2)]
        ident = sbt("ident", [128, 128])
        identb = sbt("identb", [128, 128], BF16)
        onesf = sbt("onesf", [128, 512])
        ones_bf = sbt("ones_bf", [128, 128], BF16)
        blk_bf = sbt("blk_bf", [128, 128], BF16)
        triU_f = sbt("triU_f", [128, 128])
        triU_bf = sbt("triU_bf", [128, 128], BF16)
        mask4 = sbt("mask4", [128, 4, 512], BF16)
        maskAB = sbt("maskAB", [128, 512])
        maskN2 = sbt("maskN2", [128, 256])
        resetm = sbt("resetm", [128, 512])
        ind2f = sbt("ind2f", [128, 2])
        ind2 = sbt("ind2", [128, 2], BF16)
        epsA = sbt("epsA", [128, 4])

        ARENA_BYTES = 72 * 1024
        arena = sb("arena", [128, ARENA_BYTES // 4])
        apos = [0]

        def aset(off):
            apos[0] = off

        def ar(shape, dt=F32):
            n = 1
            for d_ in shape[1:]:
                n *= d_
            nbytes = n * (2 if dt == BF16 else 4)
            nbytes = (nbytes + 31) // 32 * 32
            a0 = apos[0]
            apos[0] += nbytes
            assert apos[0] <= ARENA_BYTES, ("arena overflow", apos[0])
            v = arena[:, a0 // 4:(a0 + nbytes) // 4]
            if dt == BF16:
                v = v.bitcast(BF16)
            v = v[:, 0:n]
            if len(shape) == 3:
                v = v.rearrange("p (a b) -> p a b", b=shape[2])
            if shape[0] < 128:
                v = v[0:shape[0]]
            return v

        def art(name, shape, dt=F32):
            return Tile(ar(shape, dt), name)

        aset(0)
        mixcat_t = ar([128, 8, T], BF16)
        mixcat = [Tile(mixcat_t, "mixcat%d" % c) for c in range(8)]
        MIX0 = apos[0]
        aset(0)
        hid_t = ar([128, NJ, T], BF16)
        hidT = [Tile(hid_t, "hid%d" % j) for j in range(NJ)]
        sgt = [art("sg%d" % i, [128, T]) for i in range(2)]
        aset(MIX0)
        QTt = ar([128, 4, T], BF16)
        QT = [Tile(QTt, "QT%d" % c) for c in range(4)]
        Ebuf = [art("E%d" % i, [128, T]) for i in range(2)]
        SPb = [art("SPb%d" % i, [128, T], BF16) for i in range(2)]
        t1b = [art("t1_%d" % i, [128, T]) for i in range(2)]
        Ab = [art("A%d" % i, [128, T], BF16) for i in range(2)]
        accb = [art("acc%d" % i, [128, T]) for i in range(2)]
        Oraw_t = ar([128, 4, T])
        Oraw = [Tile(Oraw_t, "Oraw%d" % c) for c in range(4)]
        aset(MIX0)
        rawb = [art("raw%d" % i, [128, T + 8]) for i in range(2)]
        dtmp = art("dtmp", [128, T])
        txa = art("txa", [128, T], BF16)
        sxg = art("sxg", [128, 2, T], BF16)
        r32 = art("r32", [128, T]); k32 = art("k32", [128, T])
        vTb = art("vTb", [128, T], BF16)
        sig = art("sig", [128, T]); cum = art("cum", [128, T])
        eW = art("eW", [128, T]); eWx = art("eWx", [128, T]); eWi = art("eWi", [128, T])
        WC = art("WC", [128, 8])
        a32 = art("a32", [128, T]); kk32 = art("kk32", [128, T]); kkn = art("kkn", [128, T])
        kmod = art("kmod", [128, T])
        ARt = art("AR", [128, 4, 256], BF16)
        BTt = art("BT", [128, T], BF16); KTl = art("KTl", [128, T], BF16)
        rkr_t = ar([128, 4, T], BF16)
        rkr = [Tile(rkr_t, "rkr%d" % c) for c in range(4)]
        Btok = art("Btok", [128, 4, 128], BF16); Ktok = art("Ktok", [128, 4, 128], BF16)
        Vtok_t = ar([128, 4, 512], BF16)
        Vtok = [Tile(Vtok_t, "Vtok%d" % c) for c in range(4)]
        MMb = art("MMb", [128, 2, 256], BF16); KMb = art("KMb", [128, 2, 256], BF16)
        Xb = [art("X%d" % i, [128, 2, 128], BF16) for i in range(2)]
        XTb = [art("XT%d" % i, [128, 2, 128], BF16) for i in range(2)]
        SA32 = art("SA32", [128, 128]); SA16 = art("SA16", [128, 128], BF16)
        Yall_t = ar([128, 4, 512])
        Yall = [Tile(Yall_t, "Yall%d" % c) for c in range(4)]
        st8 = art("st8", [128, 6, 8])
        rwo = art("rwo", [128, 512], BF16)
        cumx = sig
        xwa32 = r32
        ysq = sig
        yn = cum
        print("arena rwkv end", apos[0])
        nrm = dtmp
        b32 = kk32
        aset(0)
        qTm_t = ar([128, 4, T], BF16)
        qTm = [Tile(qTm_t, "qTm%d" % c) for c in range(4)]
        Em = [art("Em%d" % i, [128, T], BF16) for i in range(4)]
        oTm_t = ar([128, 4, T], BF16)
        oTm = [Tile(oTm_t, "oTm%d" % c) for c in range(4)]
        rsb = art("rsb", [128, T])
        memn_t = ar([128, 8, MEM], BF16)
        memn = [Tile(memn_t, "memn%d" % c) for c in range(8)]

        banks = [Tile(st.enter_context(nc.psum_tensor("pb%d" % i, [128, 512], F32)), "pb%d" % i, excl=True)
                 for i in range(8)]
        rot = [0]

        def nb():
            b = banks[rot[0] % 6]
            rot[0] += 1
            return b

        pe_mode = [None]

        def _cls(n):
            return 32 if n <= 32 else (64 if n <= 64 else 128)

        def MM(out, lhsT, rhs, start, stop, reads, writes):
            mode = ("mm", _cls(lhsT.shape[0]), _cls(lhsT.shape[-1]), int(lhsT.base_partition()))
            sw = pe_mode[0] is not None and pe_mode[0] != mode
            if not P.mute:
                pe_mode[0] = mode
            P.op("pe", lambda e: e.matmul(out, lhsT=lhsT, rhs=rhs, start=start, stop=stop), reads, writes,
                 force_prev=sw)

        def TR(out, in_, idn, reads, writes):
            mode = ("tr", str(in_.dtype))
            sw = pe_mode[0] is not None and pe_mode[0] != mode
            if not P.mute:
                pe_mode[0] = mode
            P.op("pe", lambda e: e.transpose(out, in_, idn), reads, writes, force_prev=sw)

        def ACT(out, in_, func, reads, writes, bias=None, scale=1.0):
            if bias is None:
                P.op("act", lambda e: e.activation(out=out, in_=in_, func=func, scale=scale), reads, writes)
            else:
                P.op("act", lambda e: e.activation(out=out, in_=in_, func=func, bias=bias, scale=scale), reads, writes)

        def CP(eng, out, in_, reads, writes):
            if eng == "act":
                P.op("act", lambda e: e.copy(out=out, in_=in_), reads, writes)
            else:
                P.op(eng, lambda e: e.tensor_copy(out=out, in_=in_), reads, writes)

        def TT(eng, out, in0, in1, op, reads, writes):
            P.op(eng, lambda e: e.tensor_tensor(out=out, in0=in0, in1=in1, op=op), reads, writes)

        def STT(eng, out, in0, scalar, in1, op0, op1, reads, writes):
            P.op("dve", lambda e: e.scalar_tensor_tensor(out=out, in0=in0, scalar=scalar, in1=in1, op0=op0, op1=op1),
                 reads, writes)

        def TS(eng, out, in0, s1, s2, op0, op1, reads, writes):
            P.op(eng, lambda e: e.tensor_scalar(out=out, in0=in0, scalar1=s1, scalar2=s2, op0=op0, op1=op1),
                 reads, writes)

        def TS1(eng, out, in0, s1, op0, reads, writes):
            P.op(eng, lambda e: e.tensor_single_scalar(out=out, in_=in0, scalar=s1, op=op0), reads, writes)

        def MSET(eng, out, val, writes):
            P.op(eng, lambda e: e.memset(out, val), (), writes)

        def RECIP(out, in_, reads, writes):
            P.op("dve", lambda e: e.reciprocal(out=out, in_=in_), reads, writes)

        def ASEL(out, in_, pattern, cmp, base, cm, reads, writes):
            P.op("pool", lambda e: e.affine_select(out=out, in_=in_, pattern=pattern, compare_op=cmp, fill=0.0,
                                                   base=base, channel_multiplier=cm), reads, writes)

        def vcol(c):
            return vecs[:, c:c + 1]

        MSET("pool", onesf[:], 1.0, [onesf])
        ASEL(ident[:], onesf[:, 0:128], [[-1, 128]], ALU.is_equal, 0, 1, [onesf], [ident])
        CP("dve", identb[:], ident[:], [ident], [identb])
        CP("dve", ones_bf[:], onesf[:, 0:128], [onesf], [ones_bf])
        MSET("pool", blk_bf[:], 0.0, [blk_bf])
        MSET("pool", blk_bf[0:64, 0:64], 1.0, [blk_bf])
        MSET("pool", blk_bf[64:128, 64:128], 1.0, [blk_bf])
        ASEL(triU_f[:], onesf[:, 0:128], [[-1, 128]], ALU.is_gt, 0, 1, [onesf], [triU_f])
        CP("dve", triU_bf[:], triU_f[:], [triU_f], [triU_bf])
        for r in range(4):
            ASEL(rstd[:], onesf[:], [[1, 512]], ALU.is_gt, -128 * r, -1, [onesf], [rstd])
            CP("dve", mask4[:, r, :], rstd[:], [rstd], [mask4])
        for hh in range(2):
            ASEL(maskAB[:, hh * 256:hh * 256 + 128], onesf[:, 0:128], [[1, 128]], ALU.is_gt, 0, -1, [onesf], [maskAB])
            ASEL(maskAB[:, hh * 256 + 128:hh * 256 + 256], onesf[:, 0:128], [[1, 128]], ALU.is_ge, 0, -1, [onesf], [maskAB])
            CP("pool", maskN2[:, hh * 128:(hh + 1) * 128], triU_f[:], [triU_f], [maskN2])
        MSET("pool", resetm[:], 1.0, [resetm])
        for c in range(4):
            MSET("pool", resetm[:, c * 128:c * 128 + 1], 0.0, [resetm])
        MSET("pool", ind2f[:], 0.0, [ind2f])
        MSET("pool", ind2f[0:64, 0:1], 1.0, [ind2f])
        MSET("pool", ind2f[64:128, 1:2], 1.0, [ind2f])
        CP("dve", ind2[:], ind2f[:], [ind2f], [ind2])
        MSET("pool", epsA[:, 0:1], 1e-6, [epsA])
        MSET("pool", epsA[:, 1:2], 4e-6, [epsA])
        MSET("pool", epsA[:, 2:3], 64e-5, [epsA])
        MSET("pool", epsA[:, 3:4], 1.0, [epsA])
        P.dma("sp", [lambda e: e.dma_start(out=vecs[:], in_=vecs_d)], "vecs", writes=[vecs])
        P.dma("sp", [lambda e: e.dma_start(out=bvecs[:], in_=bvecs_d)], "bvecs", writes=[bvecs])
        P.dma("pool", [lambda e: e.dma_start(out=smallw[:], in_=small_d)], "smallw", writes=[smallw])

        wpiece = {}
        for name, kc, gc in _plan():
            o, _, _ = off[name]
            n = kc * gc
            tl = Tile(None, "w_" + name)
            wpiece[name] = tl
            P.dma("pool", [lambda e, o=o, n=n: e.dma_start(out=wb_d[:, o:o + n], in_=wf_d[:, o:o + n])],
                  ("wc", name), writes=[tl])

        wrot = [0]

        def wload(name):
            o, kc, gc = off[name]
            n = kc * gc
            s = wslot[wrot[0] % NSLOT]
            key = ("ws", wrot[0] % NSLOT)
            wrot[0] += 1
            P.dma("sp", [lambda e: e.dma_start(out=s[:, 0:n], in_=wb_d[:, o:o + n])], key,
                  reads=[wpiece[name]], writes=[s])
            return s, s[:, 0:n].rearrange("p (k c) -> p k c", c=gc)

        sqrot = [0]

        def sumsq_bcast(srcs, src_tiles, n, ones_l, nfree=T):
            pbk = nb()
            for c in range(n):
                q = sqb[sqrot[0] % 2]
                sqrot[0] += 1
                ACT(q[:, 0:nfree], srcs[c], AF.Square, [src_tiles[c]], [q])
                MM(pbk[:, 0:nfree], ones_l[:], q[:, 0:nfree], c == 0, c == n - 1, [q, ones_l], [pbk])
            return pbk

        def rmsnorm_to_xn(gcol):
            pbk = sumsq_bcast([h_t[:, c, :] for c in range(8)], hT, 8, ones_bf)
            ACT(rstd[:], pbk[:], AF.Sqrt, [pbk, epsA], [rstd], bias=epsA[:, 0:1], scale=1.0 / D)
            RECIP(rstd[:], rstd[:], [rstd], [rstd])
            for c in range(8):
                STT("dve", xn_t[:, c, :], h_t[:, c, :], vcol(gcol + c), rstd[:], ALU.mult, ALU.mult,
                    [hT[c], rstd, vecs], [xnT[c]])

        def post_residual(gcol, half):
            pbk = sumsq_bcast([tmp8[:, c, :] for c in range(8)], tmpT, 8, ones_bf)
            if half:
                ACT(rstd[:], pbk[:], AF.Sqrt, [pbk, epsA], [rstd], bias=epsA[:, 1:2], scale=4.0 / D)
            else:
                ACT(rstd[:], pbk[:], AF.Sqrt, [pbk, epsA], [rstd], bias=epsA[:, 0:1], scale=1.0 / D)
            RECIP(rstd[:], rstd[:], [rstd], [rstd])
            for c in range(8):
                STT("dve", tmp8[:, c, :], tmp8[:, c, :], vcol(gcol + c), rstd[:], ALU.mult, ALU.mult,
                    [tmpT[c], rstd, vecs], [tmpT[c]])
                TT("pool", h_t[:, c, :], h_t[:, c, :], tmp8[:, c, :], ALU.add, [hT[c], tmpT[c]], [hT[c]])

        def ffn(f, pre, post):
            BAR()
            rmsnorm_to_xn(pre)
            for i in range(NJ // 2):
                s, w = wload(f + "_in%d" % i)
                for jj in range(2):
                    j = 2 * i + jj
                    pg = nb()
                    for k in range(8):
                        MM(pg[:], w[:, k, jj * 256:jj * 256 + 128], xn_t[:, k, :], k == 0, k == 7, [s, xnT[k]], [pg])
                    pu = nb()
                    for k in range(8):
                        MM(pu[:], w[:, k, jj * 256 + 128:jj * 256 + 256], xn_t[:, k, :], k == 0, k == 7,
                           [s, xnT[k]], [pu])
                    sg = sgt[j % 2]
                    ACT(sg[:], pg[:], AF.Silu, [pg], [sg])
                    TT("dve", hid_t[:, j, :], pu[:], sg[:], ALU.mult, [pu, sg], [hidT[j]])
            for c in range(8):
                s, w = wload(f + "_out%d" % c)
                po = nb()
                for k in range(NJ):
                    MM(po[:], w[:, k, :], hid_t[:, k, :], k == 0, k == NJ - 1, [s, hidT[k]], [po])
                CP("act", tmp8[:, c, :], po[:], [po], [tmpT[c]])
            post_residual(post, True)

        arena_dma = []

        def BAR():
            P.barrier(extra=arena_dma)

        def dump(nm, ap3, tiles):
            if dbg and nm in dbg_d:
                tok = P.dma("pool", [lambda e: e.dma_start(out=dbg_d[nm][:, 0:ap3.shape[1], :], in_=ap3)], ("dbg", nm),
                            reads=tiles, final=True)
                arena_dma.append(tok)

        tile_count = 0
        for b in range(NB):
            if stages >= 4:
                BAR()
                P.dma("sp", [lambda e, b=b: e.dma_start(out=stg[:], in_=mem_d[b].rearrange("(n p) f -> p n f", p=128))],
                      "stg", writes=[stg])
                memT32 = tmp_t[:, 0:8 * MEM].rearrange("p (c t) -> p c t", t=MEM)
                for n in range(2):
                    for g in range(2):
                        pbk = nb()
                        for cc in range(4):
                            c = g * 4 + cc
                            TR(pbk[:, cc * 128:(cc + 1) * 128], stg[:, n, c * 128:(c + 1) * 128], ident[:],
                               [stg, ident], [pbk])
                        CP("act", memT32[:, g * 4:g * 4 + 4, n * 128:(n + 1) * 128],
                           pbk[:].rearrange("p (a t) -> p a t", a=4), [pbk], [tmpT[0], tmpT[1], tmpT[2], tmpT[3]])
                mt = [tmpT[0], tmpT[1], tmpT[2], tmpT[3]]
                pbk = nb()
                for c in range(8):
                    q = sqb[sqrot[0] % 2]
                    sqrot[0] += 1
                    ACT(q[:, 0:MEM], memT32[:, c, :], AF.Square, mt, [q])
                    MM(pbk[:, 0:MEM], ones_bf[:], q[:, 0:MEM], c == 0, c == 7, [q, ones_bf], [pbk])
                ACT(rstd[:, 0:MEM], pbk[:, 0:MEM], AF.Sqrt, [pbk, epsA], [rstd], bias=epsA[:, 0:1], scale=1.0 / D)
                RECIP(rstd[:, 0:MEM], rstd[:, 0:MEM], [rstd], [rstd])
                for c in range(8):
                    STT("dve", memn_t[:, c, :], memT32[:, c, :], vcol(V_MEMKV + c), rstd[:, 0:MEM], ALU.mult, ALU.mult,
                        mt + [rstd, vecs], [memn[c]])
                s, w = wload("memk")
                for c in range(4):
                    pbk = nb()
                    for k in range(8):
                        MM(pbk[:, 0:MEM], w[:, k, c * 128:(c + 1) * 128], memn_t[:, k, :], k == 0, k == 7,
                           [s, memn[k]], [pbk])
                    CP("act", kmT[:, c, :], pbk[:, 0:MEM], [pbk], [kmT])
                s, w = wload("memv")
                for mb in range(2):
                    pbk = nb()
                    for k in range(8):
                        MM(pbk[:], memn_t[:, k, mb * 128:(mb + 1) * 128], w[:, k, :], k == 0, k == 7,
                           [s, memn[k]], [pbk])
                    CP("act", vmT[:, mb, :], pbk[:], [pbk], [vmT])
            MSET("pool", S32[:], 0.0, [S32])
            MSET("pool", S16[:], 0.0, [S16])

            for it in range(NT):
                if ntiles_limit is not None and tile_count >= ntiles_limit:
                    break
                tile_count += 1
                t0 = it * T
                qb0 = it * 4
                first = (it == 0)
                dd = dbg and tile_count == 1
                for hf in range(2):
                    P.dma("sp", [lambda e, b=b, hf=hf, t0=t0: e.dma_start(
                        out=stg[:], in_=x_d[b, t0 + hf * 256:t0 + hf * 256 + 256, :].rearrange("(n p) f -> p n f", p=128))],
                        "stg", writes=[stg])
                    for n in range(2):
                        tb = hf * 2 + n
                        for g in range(2):
                            pbk = nb()
                            for cc in range(4):
                                c = g * 4 + cc
                                TR(pbk[:, cc * 128:(cc + 1) * 128], stg[:, n, c * 128:(c + 1) * 128], ident[:],
                                   [stg, ident], [pbk])
                            CP("act" if g == 0 else "dve", h_t[:, g * 4:g * 4 + 4, tb * 128:(tb + 1) * 128],
                               pbk[:].rearrange("p (a t) -> p a t", a=4), [pbk], hT[g * 4:g * 4 + 4])
                if stages >= 1:
                    ffn("f1", V_F1PRE, V_F1POST)
                if dd:
                    dump("h1", h_t[:], hT)
                if stages >= 2:
                    BAR()
                    rmsnorm_to_xn(V_MIXPRE)
                    s, w = wload("mix_q")
                    for c in range(4):
                        pbk = nb()
                        for k in range(8):
                            MM(pbk[:], w[:, k, c * 128:(c + 1) * 128], xn_t[:, k, :], k == 0, k == 7, [s, xnT[k]], [pbk])
                        P.op("act", lambda e, c=c, pbk=pbk: e.mul(out=QTt[:, c, :], in_=pbk[:], mul=0.125), [pbk], [QT[c]])
                    s, w = wload("mix_k")
                    for c in range(4):
                        pbk = nb()
                        for k in range(8):
                            MM(pbk[:], w[:, k, c * 128:(c + 1) * 128], xn_t[:, k, :], k == 0, k == 7, [s, xnT[k]], [pbk])
                        CP("dve", KTt[:, c, t0:t0 + T], pbk[:], [pbk], [KT[c]])
                    s, w = wload("mix_v")
                    for tb in range(4):
                        pbk = nb()
                        for k in range(8):
                            MM(pbk[:], xn_t[:, k, tb * 128:(tb + 1) * 128], w[:, k, :], k == 0, k == 7, [s, xnT[k]], [pbk])
                        CP("act", Vt[:, qb0 + tb, :], pbk[:], [pbk], [VT[qb0 + tb]])
                    for hd in range(8):
                        fc = hd // 2
                        po = (hd % 2) * 64
                        acc = accb[hd % 2]
                        Oacc = banks[6 + hd % 2]
                        MSET("pool", acc[:], 0.0, [acc])
                        kbs = list(range(qb0 + 3, -1, -1))
                        for idx, kb in enumerate(kbs):
                            E = Ebuf[idx % 2]; SP = SPb[idx % 2]; t1 = t1b[idx % 2]; A = Ab[idx % 2]
                            pz = nb()
                            MM(pz[:], KTt[po:po + 64, fc, kb * 128:(kb + 1) * 128], QTt[po:po + 64, fc, :], True, True,
                               [KT[fc], QT[fc]], [pz])
                            ACT(E[:], pz[:], AF.Exp, [pz], [E])
                            ACT(SP[:], E[:], AF.Ln, [E, epsA], [SP], bias=epsA[:, 3:4])
                            diag = kb >= qb0
                            if diag:
                                r = kb - qb0
                                TT("pool", SP[:], SP[:], mask4[:, r, :], ALU.mult, [SP, mask4], [SP])
                            pw = nb()
                            MM(pw[:], triU_bf[:], SP[:], True, True, [triU_bf, SP], [pw])
                            pc = nb()
                            MM(pc[:], ones_bf[:], SP[:], True, True, [ones_bf, SP], [pc])
                            TT("dve", t1[:], pz[:], SP[:], ALU.subtract, [pz, SP], [t1])
                            TT("dve", t1[:], t1[:], pw[:], ALU.subtract, [t1, pw], [t1])
                            TT("pool", t1[:], t1[:], acc[:], ALU.subtract, [t1, acc], [t1])
                            TT("dve", acc[:], acc[:], pc[:], ALU.add, [acc, pc], [acc])
                            ACT(A[:], t1[:], AF.Exp, [t1], [A])
                            if diag:
                                TT("pool", A[:], A[:], mask4[:, r, :], ALU.mult, [A, mask4], [A])
                            MM(Oacc[:], Vt[:, kb, fc * 128:(fc + 1) * 128], A[:], idx == 0, idx == len(kbs) - 1,
                               [VT[kb], A], [Oacc])
                        CP("act", Oraw_t[po:po + 64, fc, :], Oacc[po:po + 64, :], [Oacc], [Oraw[fc]])
                    for fc in range(4):
                        q = sqb[sqrot[0] % 2]
                        sqrot[0] += 1
                        ACT(q[:], Oraw_t[:, fc, :], AF.Square, [Oraw[fc]], [q])
                        pbk = nb()
                        MM(pbk[:], blk_bf[:], q[:], True, True, [blk_bf, q], [pbk])
                        ACT(rstd[:], pbk[:], AF.Sqrt, [pbk, epsA], [rstd], bias=epsA[:, 0:1], scale=1.0 / 64)
                        RECIP(rstd[:], rstd[:], [rstd], [rstd])
                        STT("dve", mixcat_t[:, fc, :], Oraw_t[:, fc, :], vcol(V_SBG + fc), rstd[:], ALU.mult, ALU.mult,
                            [Oraw[fc], rstd, vecs], [mixcat[fc]])

                    P.mute = stages < 3
                    BAR()
                    rawi = [0]

                    def lerp(ps_ap, ps_tile, M, ci, mucol, dest, dest_tile):
                        raw = rawb[rawi[0] % 2]
                        rawi[0] += 1
                        if first:
                            MSET("pool", raw[0:M, 0:1], 0.0, [raw])
                        else:
                            CP("pool", raw[0:M, 0:1], carry[0:M, ci:ci + 1], [carry], [raw])
                        CP("act", raw[0:M, 1:T + 1], ps_ap, [ps_tile], [raw])
                        CP("pool", carry[0:M, ci:ci + 1], raw[0:M, T:T + 1], [raw], [carry])
                        TT("dve", dtmp[0:M, :], raw[0:M, 0:T], raw[0:M, 1:T + 1], ALU.subtract, [raw], [dtmp])
                        STT("dve", dest, dtmp[0:M, :], vecs[0:M, mucol:mucol + 1], raw[0:M, 1:T + 1], ALU.mult, ALU.add,
                            [dtmp, raw, vecs], [dest_tile])

                    s, w = wload("mix_lora")
                    pbk = nb()
                    for k in range(8):
                        MM(pbk[:], w[:, k, 0:128], xn_t[:, k, :], k == 0, k == 7, [s, xnT[k]], [pbk])
                    lerp(pbk[:], pbk, 128, 12, V_MU_XWA, xwa32[:], xwa32)
                    ACT(txa[0:64, :], xwa32[0:64, :], AF.Tanh, [xwa32], [txa])
                    CP("pool", txa[64:128, :], xwa32[64:128, :], [xwa32], [txa])
                    pbk = nb()
                    for k in range(8):
                        MM(pbk[:], w[:, k, 128:256], xn_t[:, k, :], k == 0, k == 7, [s, xnT[k]], [pbk])
                    lerp(pbk[:], pbk, 128, 13, V_MU_XG0, xwa32[:], xwa32)
                    ACT(sxg[:, 0, :], xwa32[:], AF.Sigmoid, [xwa32], [sxg])
                    pbk = nb()
                    for k in range(8):
                        MM(pbk[0:32, :], w[:, k, 256:288], xn_t[:, k, :], k == 0, k == 7, [s, xnT[k]], [pbk])
                    lerp(pbk[0:32, :], pbk, 32, 14, V_MU_XG1, xwa32[0:32, :], xwa32)
                    ACT(sxg[0:32, 1, :], xwa32[0:32, :], AF.Sigmoid, [xwa32], [sxg])

                    for fc in range(4):
                        P.mute = stages < 3 or rwsub < 2
                        s, w = wload("mix_rw%d" % fc)
                        for which, dest, dtile, ci, mucol in ((0, r32[:], r32, fc, V_MU_R + fc),
                                                              (1, k32[:], k32, 4 + fc, V_MU_K + fc),
                                                              (2, vTb[:], vTb, 8 + fc, V_MU_V + fc)):
                            pbk = nb()
                            for k in range(8):
                                MM(pbk[:], w[:, k, which * 128:(which + 1) * 128], xn_t[:, k, :], k == 0, k == 7,
                                   [s, xnT[k]], [pbk])
                            lerp(pbk[:], pbk, 128, ci, mucol, dest, dtile)
                        cs = slice(fc * 128, (fc + 1) * 128)
                        pbk = nb()
                        MM(pbk[:], smallw[0:64, 0, cs], txa[0:64, :], True, True, [smallw, txa], [pbk])
                        ACT(sig[:], pbk[:], AF.Sigmoid, [pbk, vecs], [sig], bias=vcol(V_W0 + fc))
                        P.op("dve", lambda e: e.tensor_tensor_scan(out=cum[:], data0=resetm[:], data1=sig[:], initial=0.0,
                                                                   op0=ALU.mult, op1=ALU.add), [resetm, sig], [cum])
                        TT("pool", cumx[:], cum[:], sig[:], ALU.subtract, [cum, sig], [sig])
                        ACT(eW[:], cum[:], AF.Exp, [cum], [eW], scale=-CDEC)
                        ACT(eWx[:], cumx[:], AF.Exp, [cumx], [eWx], scale=-CDEC)
                        ACT(eWi[:], cum[:], AF.Exp, [cum], [eWi], scale=CDEC)
                        CP("pool", WC[:, 0:4], eW[:].rearrange("p (c t) -> p c t", t=128)[:, :, 127], [eW], [WC])
                        pbk = nb()
                        MM(pbk[:], smallw[64:128, 0, cs], txa[64:128, :], True, True, [smallw, txa], [pbk])
                        ACT(a32[:], pbk[:], AF.Sigmoid, [pbk, vecs], [a32], bias=vcol(V_A0 + fc))
                        TS1("dve", kk32[:], k32[:], vcol(V_KK + fc), ALU.mult, [k32, vecs], [kk32])
                        q = sqb[sqrot[0] % 2]
                        sqrot[0] += 1
                        ACT(q[:], kk32[:], AF.Square, [kk32], [q])
                        pbk = nb()
                        MM(pbk[:], blk_bf[:], q[:], True, True, [blk_bf, q], [pbk])
                        ACT(nrm[:], pbk[:], AF.Sqrt, [pbk], [nrm])
                        TS1("dve", nrm[:], nrm[:], 1e-12, ALU.max, [nrm], [nrm])
                        RECIP(nrm[:], nrm[:], [nrm], [nrm])
                        TT("dve", kkn[:], kk32[:], nrm[:], ALU.mult, [kk32, nrm], [kkn])
                        TS("pool", kmod[:], a32[:], vcol(V_KA + fc), vcol(V_KA + fc), ALU.mult, ALU.subtract, [a32, vecs], [kmod])
                        STT("pool", kmod[:], kmod[:], 1.0, k32[:], ALU.add, ALU.mult, [kmod, k32], [kmod])
                        ar4 = ARt[:]
                        STT("dve", ar4[:, :, 0:128], kkn[:].rearrange("p (c t) -> p c t", t=128), -1.0,
                            eWx[:].rearrange("p (c t) -> p c t", t=128), ALU.mult, ALU.mult, [kkn, eWx], [ARt])
                        TT("pool", ar4[:, :, 128:256], r32[:].rearrange("p (c t) -> p c t", t=128),
                           eW[:].rearrange("p (c t) -> p c t", t=128), ALU.mult, [r32, eW], [ARt])
                        TT("dve", b32[:], kkn[:], a32[:], ALU.mult, [kkn, a32], [b32])
                        TT("dve", BTt[:], b32[:], eWi[:], ALU.mult, [b32, eWi], [BTt])
                        TT("pool", KTl[:], kmod[:], eWi[:], ALU.mult, [kmod, eWi], [KTl])
                        STT("dve", rkr_t[:, fc, :], r32[:], vcol(V_RK + fc), kmod[:], ALU.mult, ALU.mult,
                            [r32, kmod, vecs], [rkr[fc]])
                        P.mute = stages < 3 or rwsub < 3
                        for src, stile, dst_ap, dtile in ((BTt, BTt, Btok[:], Btok), (KTl, KTl, Ktok[:], Ktok),
                                                          (vTb, vTb, Vtok_t[:, :, cs], Vtok[fc])):
                            pbk = nb()
                            pbb = pbk[:].bitcast(BF16)
                            for tc in range(4):
                                TR(pbb[:, tc * 128:(tc + 1) * 128], src[:, tc * 128:(tc + 1) * 128], identb[:],
                                   [stile, identb], [pbk])
                            CP("act", dst_ap, pbb[:, 0:512].rearrange("p (c f) -> p c f", f=128), [pbk], [dtile])
                        P.mute = stages < 3 or rwsub < 4
                        for tc in range(4):
                            tcs = slice(tc * 128, (tc + 1) * 128)
                            pm = nb(); pk = nb(); pn = nb()
                            for hh in range(2):
                                po = hh * 64
                                MM(pm[:, hh * 256:(hh + 1) * 256], BTt[po:po + 64, tcs], ARt[po:po + 64, tc, :], True, True,
                                   [BTt, ARt], [pm])
                                MM(pk[:, hh * 256:(hh + 1) * 256], KTl[po:po + 64, tcs], ARt[po:po + 64, tc, :], True, True,
                                   [KTl, ARt], [pk])
                                MM(pn[:, hh * 128:(hh + 1) * 128], ARt[po:po + 64, tc, 0:128], BTt[po:po + 64, tcs], True, True,
                                   [ARt, BTt], [pn])
                            TT("dve", MMb[:].rearrange("p a b -> p (a b)"), pm[:], maskAB[:], ALU.mult, [pm, maskAB], [MMb])
                            TT("dve", KMb[:].rearrange("p a b -> p (a b)"), pk[:], maskAB[:], ALU.mult, [pk, maskAB], [KMb])
                            X = Xb[0]; XT = XTb[0]
                            TT("dve", X[:].rearrange("p a b -> p (a b)"), pn[:, 0:256], maskN2[:], ALU.mult, [pn, maskN2], [X])
                            CP("pool", XT[:], MMb[:, :, 0:128], [MMb], [XT])
                            pr = nb()
                            MM(pr[:, 0:128], ARt[:, tc, 0:128], S16[:, fc, :], True, False, [ARt, S16], [pr])
                            for hh in range(2):
                                MM(pr[:, hh * 64:(hh + 1) * 64], KMb[:, hh, 0:128],
                                   Vtok_t[:, tc, fc * 128 + hh * 64:fc * 128 + hh * 64 + 64], False, hh == 1,
                                   [KMb, Vtok[fc]], [pr])
                            CP("dve", SA16[:], pr[:, 0:128], [pr], [SA16])
                            CP("dve", SA32[:], pr[:, 0:128], [pr], [SA32])
                            for i in range(7):
                                X = Xb[i % 2]; XT = XTb[i % 2]
                                pd = nb()
                                for hh in range(2):
                                    MM(pd[:, hh * 64:(hh + 1) * 64], XT[:, hh, :], SA16[:, hh * 64:(hh + 1) * 64], True, True,
                                       [XT, SA16], [pd])
                                TT("dve", SA16[:], SA32[:], pd[:, 0:128], ALU.add, [SA32, pd], [SA16])
                                TT("dve", SA32[:], SA32[:], pd[:, 0:128], ALU.add, [SA32, pd], [SA32])
                                if i < 6:
                                    Xn = Xb[(i + 1) % 2]; XTn = XTb[(i + 1) % 2]
                                    px = nb(); pxt = nb()
                                    for hh in range(2):
                                        MM(px[:, hh * 128:(hh + 1) * 128], XT[:, hh, :], X[:, hh, :], True, True, [XT, X], [px])
                                        MM(pxt[:, hh * 128:(hh + 1) * 128], X[:, hh, :], XT[:, hh, :], True, True, [XT, X], [pxt])
                                    CP("pool" if False else "dve", Xn[:].rearrange("p a b -> p (a b)"), px[:, 0:256], [px], [Xn])
                                    CP("act", XTn[:].rearrange("p a b -> p (a b)"), pxt[:, 0:256], [pxt], [XTn])
                            py = nb()
                            MM(py[:, 0:128], ARt[:, tc, 128:256], S16[:, fc, :], True, False, [ARt, S16], [py])
                            for hh in range(2):
                                hs = slice(hh * 64, (hh + 1) * 64)
                                MM(py[:, hs], MMb[:, hh, 128:256], SA16[:, hs], False, False, [MMb, SA16], [py])
                                MM(py[:, hs], KMb[:, hh, 128:256], Vtok_t[:, tc, fc * 128 + hh * 64:fc * 128 + hh * 64 + 64],
                                   False, hh == 1, [KMb, Vtok[fc]], [py])
                            CP("act", Yall_t[:, tc, cs], py[:, 0:128], [py], [Yall[tc]])
                            pst = nb()
                            MM(pst[:, 0:128], Btok[:, tc, :], SA16[:], True, False, [Btok, SA16], [pst])
                            MM(pst[:, 0:128], Ktok[:, tc, :], Vtok_t[:, tc, cs], False, True, [Ktok, Vtok[fc]], [pst])
                            for hh in range(2):
                                ps_ = slice(hh * 64, (hh + 1) * 64)
                                TS1("pool", S32[ps_, fc, ps_], S32[ps_, fc, ps_], WC[ps_, tc:tc + 1], ALU.mult, [S32, WC], [S32])
                                STT("dve", S32[ps_, fc, ps_], pst[ps_, ps_], WC[ps_, tc:tc + 1], S32[ps_, fc, ps_],
                                    ALU.mult, ALU.add, [pst, WC, S32], [S32])
                            CP("act", S16[:, fc, :], S32[:, fc, :], [S32], [S16])
                    P.mute = stages < 3 or rwsub < 5
                    for tc in range(4):
                        tcs = slice(tc * 128, (tc + 1) * 128)
                        pg = nb()
                        MM(pg[:], sxg[:, 0, tcs], smallw[:, 1, :], True, False, [sxg, smallw], [pg])
                        MM(pg[:], sxg[0:32, 1, tcs], smallw[0:32, 2, :], False, True, [sxg, smallw], [pg])
                        pbn = nb()
                        for fc in range(4):
                            MM(pbn[:, fc * 2:fc * 2 + 2], rkr_t[:, fc, tcs], ind2[:], True, True, [rkr[fc], ind2], [pbn])
                        CP("act", st8[:, 5, :], pbn[:, 0:8], [pbn], [st8])
                        Y = Yall_t[:, tc, :]
                        Y3 = Y.rearrange("p (h d) -> p h d", d=64)
                        P.op("dve", lambda e, Y3=Y3: e.tensor_reduce(out=st8[:, 0, :], in_=Y3, axis=AX.X, op=ALU.add),
                             [Yall[tc]], [st8])
                        ACT(ysq[:], Y, AF.Square, [Yall[tc]], [ysq])
                        P.op("dve", lambda e: e.tensor_reduce(out=st8[:, 1, :], in_=ysq[:].rearrange("p (h d) -> p h d", d=64),
                                                              axis=AX.X, op=ALU.add), [ysq], [st8])
                        TS1("dve", st8[:, 2, :], st8[:, 0, :], 1.0 / 64, ALU.mult, [st8], [st8])
                        TT("dve", st8[:, 3, :], st8[:, 2, :], st8[:, 2, :], ALU.mult, [st8], [st8])
                        STT("dve", st8[:, 3, :], st8[:, 1, :], 1.0 / 64, st8[:, 3, :], ALU.mult, ALU.subtract, [st8], [st8])
                        ACT(st8[:, 4, :], st8[:, 3, :], AF.Sqrt, [st8, epsA], [st8], bias=epsA[:, 2:3])
                        RECIP(st8[:, 4, :], st8[:, 4, :], [st8], [st8])
                        for hd in range(8):
                            hs = slice(hd * 64, (hd + 1) * 64)
                            TS("dve" if hd % 2 == 0 else "pool", yn[:, hs], Yall_t[:, tc, hs], st8[:, 2, hd:hd + 1],
                               st8[:, 4, hd:hd + 1], ALU.subtract, ALU.mult, [Yall[tc], st8], [yn])
                        TT("dve", yn[:], yn[:], bvecs[:, 0:512], ALU.mult, [yn, bvecs], [yn])
                        TT("pool", yn[:], yn[:], bvecs[:, 512:1024], ALU.add, [yn, bvecs], [yn])
                        for hd in range(8):
                            hs = slice(hd * 64, (hd + 1) * 64)
                            STT("dve" if hd % 2 == 0 else "pool", yn[:, hs], Vtok_t[:, tc, hs], st8[:, 5, hd:hd + 1], yn[:, hs],
                                ALU.mult, ALU.add, [Vtok[hd // 2], st8, yn], [yn])
                        TT("dve", rwo[:], pg[:], yn[:], ALU.mult, [pg, yn], [rwo])
                        pbk = nb()
                        pbb = pbk[:].bitcast(BF16)
                        for fc in range(4):
                            TR(pbb[:, fc * 128:(fc + 1) * 128], rwo[:, fc * 128:(fc + 1) * 128], identb[:], [rwo, identb], [pbk])
                        CP("act", mixcat_t[:, 4:8, tcs], pbb[:, 0:512].rearrange("p (c t) -> p c t", t=128), [pbk], mixcat[4:8])
                    P.mute = False
                    if dd:
                        dump("sbo", mixcat_t[:], mixcat)
                    for g in range(2):
                        s, w = wload("mixo%d" % g)
                        for cc in range(4):
                            c = g * 4 + cc
                            pbk = nb()
                            for k in range(8):
                                MM(pbk[:], w[:, k, cc * 128:(cc + 1) * 128], mixcat_t[:, k, :], k == 0, k == 7,
                                   [s, mixcat[k]], [pbk])
                            CP("act", tmp8[:, c, :], pbk[:], [pbk], [tmpT[c]])
                    post_residual(V_MIXPOST, False)
                if dd:
                    dump("h2", h_t[:], hT)
                if stages >= 4:
                    BAR()
                    rmsnorm_to_xn(V_MEMPRE)
                    s, w = wload("memq")
                    for c in range(4):
                        pbk = nb()
                        for k in range(8):
                            MM(pbk[:], w[:, k, c * 128:(c + 1) * 128], xn_t[:, k, :], k == 0, k == 7, [s, xnT[k]], [pbk])
                        P.op("act", lambda e, c=c, pbk=pbk: e.mul(out=qTm_t[:, c, :], in_=pbk[:], mul=128.0 ** -0.5),
                             [pbk], [qTm[c]])
                    for hd in range(4):
                        es = []
                        for mb in range(2):
                            pbk = nb()
                            MM(pbk[:], kmT[:, hd, mb * 128:(mb + 1) * 128], qTm_t[:, hd, :], True, True, [kmT, qTm[hd]], [pbk])
                            Et = Em[(hd % 2) * 2 + mb]
                            ACT(Et[:], pbk[:], AF.Exp, [pbk], [Et])
                            es.append(Et)
                        pS = nb()
                        MM(pS[:], ones_bf[:], es[0][:], True, False, [ones_bf, es[0]], [pS])
                        MM(pS[:], ones_bf[:], es[1][:], False, True, [ones_bf, es[1]], [pS])
                        pO = nb()
                        MM(pO[:], vmT[:, 0, hd * 128:(hd + 1) * 128], es[0][:], True, False, [vmT, es[0]], [pO])
                        MM(pO[:], vmT[:, 1, hd * 128:(hd + 1) * 128], es[1][:], False, True, [vmT, es[1]], [pO])
                        RECIP(rsb[:], pS[:], [pS], [rsb])
                        TT("dve", oTm_t[:, hd, :], pO[:], rsb[:], ALU.mult, [pO, rsb], [oTm[hd]])
                    s, w = wload("memo")
                    for c in range(8):
                        pbk = nb()
                        for k in range(4):
                            MM(pbk[:], w[:, k, c * 128:(c + 1) * 128], oTm_t[:, k, :], k == 0, k == 3, [s, oTm[k]], [pbk])
                        CP("act", tmp8[:, c, :], pbk[:], [pbk], [tmpT[c]])
                    post_residual(V_MEMPOST, False)
                if dd:
                    dump("h3", h_t[:], hT)
                if stages >= 5:
                    ffn("f2", V_F2PRE, V_F2POST)
                for tb in range(4):
                    for g in range(2):
                        pbk = nb()
                        for cc in range(4):
                            c = g * 4 + cc
                            TR(pbk[:, cc * 128:(cc + 1) * 128], h_t[:, c, tb * 128:(tb + 1) * 128], ident[:],
                               [hT[c], ident], [pbk])
                        CP("act" if g == 0 else "dve", ost[:, tb, g * 512:(g + 1) * 512], pbk[:], [pbk],
                           [tmpT[2 * tb], tmpT[2 * tb + 1]])
                P.dma("sp", [lambda e, b=b, t0=t0: e.dma_start(
                    out=out_d[b, t0:t0 + T, :].rearrange("(n p) f -> p n f", p=128), in_=ost)], "ost",
                    reads=tmpT, final=True)
        P.emit()
    return nc


_CACHE = {}


def kernel(**inputs):
    inp = {k: np.asarray(v) for k, v in inputs.items()}
    wf = _host_weights(inp)
    vecs = _host_vecs(inp)
    small = _host_small(inp)
    bv = _host_bvecs(inp)
    if "nc" not in _CACHE:
        _CACHE["nc"] = build_nc()
    nc = _CACHE["nc"]
    in_maps = []
    for c in range(8):
        in_maps.append({
            "x": np.ascontiguousarray(inp["x"][c * NB:(c + 1) * NB]),
            "mem": np.ascontiguousarray(inp["mem"][c * NB:(c + 1) * NB]),
            "wf": wf, "vecs": vecs, "small": small, "bvecs": bv,
        })
    res = run_bass_kernel_spmd(nc, in_maps, core_ids=list(range(8)))
    out = np.concatenate([np.asarray(r["out"]) for r in res.results], axis=0)
    return out.astype(np.float32)
```
